# Optimizing a Trainium2 kernel written in Bass

```python
import math
import jax, jax.numpy as jnp
from jax import lax
import numpy as np

D_MODEL = 1024
BATCH = 16
SEQ = 256
DEPTH = 4
DEC_BATCH = 2
DEC_SEQ = 2048
PAST_LEN = 256

GRID_W = 64
N_MIXERS = 2
N_HYENA = (DEPTH + 1) // 2
N_ATTN = DEPTH // 2
N_HEADS = 8
HEAD_DIM = D_MODEL // (2 * N_HEADS)
V_DIM = 2 * HEAD_DIM
ROPE_THETA = 10000.0
ROT_FREQS = HEAD_DIM // 4
Q_BLOCK = 128
HYENA_ORDER = 2
SHORT_W = 3
EMB_BANDS = 16
EMB_DIM = 1 + 2 * EMB_BANDS
FILTER_HIDDEN = 64
DECAY_TARGET = 1e-2
DECAY_PCT_SHORT = 0.3
DECAY_PCT_LONG = 1.5
D_FF = 2816
EPS = 1e-6

kernel_name = "hyena_diffattn_prefix_dit_step"


def rmsnorm(x, w):
    x32 = x.astype(jnp.float32)
    y = x32 * lax.rsqrt(jnp.mean(x32 * x32, axis=-1, keepdims=True) + EPS)
    return (y * w.astype(jnp.float32)).astype(x.dtype)


def adaln_mod(cond, w_ada, b_ada):
    m = jax.nn.silu(cond) @ w_ada + b_ada
    return jnp.split(m[:, None, :], 6, axis=-1)


def modulate(h, shift, scale):
    return h * (1.0 + scale) + shift


def dwconv3(x, w, b):
    xp = jnp.pad(x, ((0, 0), (1, 1), (0, 0)))
    return xp[:, :-2] * w[0] + xp[:, 1:-1] * w[1] + xp[:, 2:] * w[2] + b


def hyena_filter(L, f_w1, f_b1, f_freq, f_w2, f_b2, f_w3):
    t = jnp.linspace(0.0, 1.0, L, dtype=jnp.float32)[:, None]
    w = 2.0 * math.pi * jnp.arange(L, dtype=jnp.float32)[:, None] / L
    bands = jnp.linspace(1e-4, EMB_BANDS - 1, EMB_BANDS, dtype=jnp.float32)
    z = jnp.concatenate([t, jnp.cos(bands * w), -jnp.sin(bands * w)], axis=-1)
    z = z.astype(f_w1.dtype)
    hid = jnp.sin(f_freq * (z @ f_w1 + f_b1))
    hid = jnp.sin(f_freq * (hid @ f_w2 + f_b2))
    h = (hid @ f_w3).astype(jnp.float32)
    min_decay = math.log(DECAY_TARGET) / DECAY_PCT_LONG
    max_decay = math.log(DECAY_TARGET) / DECAY_PCT_SHORT
    deltas = jnp.linspace(min_decay, max_decay, D_MODEL, dtype=jnp.float32)
    decay = jnp.exp(-t * jnp.abs(deltas))
    h_fwd = h[:, :D_MODEL] * decay
    h_bwd = h[:, D_MODEL:] * decay
    zero = jnp.zeros((1, D_MODEL), jnp.float32)
    return jnp.concatenate([h_fwd, zero, h_bwd[1:][::-1]], axis=0)


def long_conv(u, k_circ, d_bias):
    L = u.shape[1]
    uf = jnp.fft.rfft(u.astype(jnp.float32), n=2 * L, axis=1)
    kf = jnp.fft.rfft(k_circ, n=2 * L, axis=0)
    y = jnp.fft.irfft(uf * kf[None], n=2 * L, axis=1)[:, :L]
    y = y + u.astype(jnp.float32) * d_bias.astype(jnp.float32)
    return y.astype(u.dtype)


def hyena_mixer(h, w_in, b_in, w_short, b_short, f_w1, f_b1, f_freq, f_w2, f_b2, f_w3,
                d_bias, w_out, b_out):
    L = h.shape[1]
    u = dwconv3(h @ w_in + b_in, w_short, b_short)
    x0, x1, v = jnp.split(u, HYENA_ORDER + 1, axis=-1)
    k_circ = hyena_filter(L, f_w1, f_b1, f_freq, f_w2, f_b2, f_w3)
    y = x0 * long_conv(v * x1, k_circ, d_bias)
    return y @ w_out + b_out


def axial_rope_tables(rows, dtype):
    r = jnp.repeat(jnp.arange(rows, dtype=jnp.float32), GRID_W)
    cidx = jnp.tile(jnp.arange(GRID_W, dtype=jnp.float32), rows)
    inv = ROPE_THETA ** (-jnp.arange(ROT_FREQS, dtype=jnp.float32) / ROT_FREQS)
    ar = r[:, None] * inv
    ac = cidx[:, None] * inv
    cos = jnp.concatenate([jnp.cos(ar), jnp.cos(ar), jnp.cos(ac), jnp.cos(ac)], axis=-1)
    sin = jnp.concatenate([jnp.sin(ar), jnp.sin(ar), jnp.sin(ac), jnp.sin(ac)], axis=-1)
    return cos[:, None, :].astype(dtype), sin[:, None, :].astype(dtype)


def rope_2d(x, cos, sin):
    xa = x.reshape(x.shape[:-1] + (2, 2, ROT_FREQS))
    x1, x2 = xa[..., 0, :], xa[..., 1, :]
    rot = jnp.stack([-x2, x1], axis=-2).reshape(x.shape)
    return x * cos + rot * sin


def qkv_heads(h, w_qkv):
    B, L, _ = h.shape
    qkv = h @ w_qkv
    q, k, v = jnp.split(qkv, 3, axis=-1)
    q = q.reshape(B, L, N_HEADS, 2, HEAD_DIM).transpose(0, 2, 1, 3, 4)
    k = k.reshape(B, L, N_HEADS, 2, HEAD_DIM).transpose(0, 2, 1, 3, 4)
    v = v.reshape(B, L, N_HEADS, V_DIM).transpose(0, 2, 1, 3)
    return q, k, v


def diff_attention(q, k, v, lam, lam_init, subln_w):
    B, H, Lq = q.shape[:3]
    nb = Lq // Q_BLOCK
    scale = HEAD_DIM ** -0.5
    k1, k2 = k[:, :, :, 0], k[:, :, :, 1]
    qb = q.reshape(B, H, nb, Q_BLOCK, 2, HEAD_DIM).transpose(2, 0, 1, 3, 4, 5)

    def block(qblk):
        s1 = jnp.einsum('bhqd,bhkd->bhqk', qblk[:, :, :, 0], k1).astype(jnp.float32) * scale
        s2 = jnp.einsum('bhqd,bhkd->bhqk', qblk[:, :, :, 1], k2).astype(jnp.float32) * scale
        p = jax.nn.softmax(s1, axis=-1) - lam * jax.nn.softmax(s2, axis=-1)
        return jnp.einsum('bhqk,bhkd->bhqd', p.astype(v.dtype), v)

    o = lax.map(block, qb)
    o = o.transpose(1, 2, 0, 3, 4).reshape(B, H, Lq, V_DIM)
    o = rmsnorm(o, subln_w) * (1.0 - lam_init)
    return o.transpose(0, 2, 1, 3).reshape(B, Lq, N_HEADS * V_DIM)


def conv_ffn(h, w_up, w_dw, b_dw, w_down):
    u = dwconv3(h @ w_up, w_dw, b_dw)
    g, val = jnp.split(u, 2, axis=-1)
    return (jax.nn.silu(g) * val) @ w_down


def setup_inputs(seed: int = 0) -> dict:
    key = jax.random.key(seed)
    ks = iter(jax.random.split(key, 40))
    D = D_MODEL

    def nrm(shape, s):
        return jax.random.normal(next(ks), shape, jnp.float32) * s

    return {
        "x_prompt": nrm((BATCH, SEQ, D), 1.0),
        "x_sample": nrm((DEC_BATCH, DEC_SEQ, D), 1.0),
        "cache_k": nrm((DEC_BATCH, N_ATTN, N_HEADS, PAST_LEN, V_DIM), 1.0),
        "cache_v": nrm((DEC_BATCH, N_ATTN, N_HEADS, PAST_LEN, V_DIM), 1.0),
        "c": nrm((DEC_BATCH, D), 1.0),
        "c_ctx": nrm((D,), 1.0),
        "w_ada": nrm((DEPTH, D, 6 * D), 0.5 * D ** -0.5),
        "b_ada": nrm((DEPTH, 6 * D), 0.01),
        "norm_w": 1.0 + nrm((DEPTH, 4, D), 0.05),
        "hy_w_in": nrm((N_HYENA, D, 3 * D), D ** -0.5),
        "hy_b_in": nrm((N_HYENA, 3 * D), 0.01),
        "hy_w_short": nrm((N_HYENA, SHORT_W, 3 * D), SHORT_W ** -0.5),
        "hy_b_short": nrm((N_HYENA, 3 * D), 0.01),
        "hy_f_w1": nrm((N_HYENA, EMB_DIM, FILTER_HIDDEN), EMB_DIM ** -0.5),
        "hy_f_b1": nrm((N_HYENA, FILTER_HIDDEN), 0.1),
        "hy_f_freq": 1.0 + nrm((N_HYENA, FILTER_HIDDEN), 0.05),
        "hy_f_w2": nrm((N_HYENA, FILTER_HIDDEN, FILTER_HIDDEN), FILTER_HIDDEN ** -0.5),
        "hy_f_b2": nrm((N_HYENA, FILTER_HIDDEN), 0.1),
        "hy_f_w3": nrm((N_HYENA, FILTER_HIDDEN, 2 * D), FILTER_HIDDEN ** -0.5),
        "hy_d_bias": nrm((N_HYENA, D), 0.5),
        "hy_w_out": nrm((N_HYENA, D, D), D ** -0.5),
        "hy_b_out": nrm((N_HYENA, D), 0.01),
        "at_w_qkv": nrm((N_ATTN, D, 3 * D), D ** -0.5),
        "at_w_out": nrm((N_ATTN, D, D), D ** -0.5),
        "at_lambda_q1": nrm((N_ATTN, HEAD_DIM), 0.1),
        "at_lambda_k1": nrm((N_ATTN, HEAD_DIM), 0.1),
        "at_lambda_q2": nrm((N_ATTN, HEAD_DIM), 0.1),
        "at_lambda_k2": nrm((N_ATTN, HEAD_DIM), 0.1),
        "at_subln": 1.0 + nrm((N_ATTN, V_DIM), 0.05),
        "ffn_w_up": nrm((DEPTH, D, 2 * D_FF), D ** -0.5),
        "ffn_w_dw": nrm((DEPTH, 3, 2 * D_FF), 3 ** -0.5),
        "ffn_b_dw": nrm((DEPTH, 2 * D_FF), 0.01),
        "ffn_w_down": nrm((DEPTH, D_FF, D), D_FF ** -0.5),
    }


def reference(x_prompt, x_sample, cache_k, cache_v, c, c_ctx, w_ada, b_ada, norm_w,
              hy_w_in, hy_b_in, hy_w_short, hy_b_short, hy_f_w1, hy_f_b1, hy_f_freq,
              hy_f_w2, hy_f_b2, hy_f_w3, hy_d_bias, hy_w_out, hy_b_out,
              at_w_qkv, at_w_out, at_lambda_q1, at_lambda_k1, at_lambda_q2, at_lambda_k2,
              at_subln, ffn_w_up, ffn_w_dw, ffn_b_dw, ffn_w_down):
    ROWS = x_sample.shape[1] // GRID_W
    cos_s, sin_s = axial_rope_tables(ROWS, x_sample.dtype)
    xp, xs = x_prompt, x_sample
    Bp, Lp = xp.shape[:2]
    Bs, Ls = xs.shape[:2]
    Lc = cache_k.shape[3]
    new_k, new_v = [], []
    for i in range(DEPTH):
        mp = adaln_mod(c_ctx[None, :], w_ada[i], b_ada[i])
        ms = adaln_mod(c, w_ada[i], b_ada[i])
        j = i // N_MIXERS
        hp = modulate(rmsnorm(xp, norm_w[i, 0]), mp[0], mp[1])
        hs = modulate(rmsnorm(xs, norm_w[i, 0]), ms[0], ms[1])
        if i % N_MIXERS == 0:
            hy = (hy_w_in[j], hy_b_in[j], hy_w_short[j], hy_b_short[j], hy_f_w1[j],
                  hy_f_b1[j], hy_f_freq[j], hy_f_w2[j], hy_f_b2[j], hy_f_w3[j],
                  hy_d_bias[j], hy_w_out[j], hy_b_out[j])
            yp = hyena_mixer(hp, *hy)
            ys = hyena_mixer(hs, *hy)
        else:
            lam_init = 0.8 - 0.6 * math.exp(-0.3 * i)
            lam = (jnp.exp(jnp.sum(at_lambda_q1[j] * at_lambda_k1[j]).astype(jnp.float32))
                   - jnp.exp(jnp.sum(at_lambda_q2[j] * at_lambda_k2[j]).astype(jnp.float32))
                   + lam_init)
            qp, kp, vp = qkv_heads(hp, at_w_qkv[j])
            new_k.append(kp.reshape(Bp, N_HEADS, Lp, V_DIM))
            new_v.append(vp)
            yp = diff_attention(qp, kp, vp, lam, lam_init, at_subln[j]) @ at_w_out[j]
            qs, ks_, vs = qkv_heads(hs, at_w_qkv[j])
            qs = rope_2d(qs, cos_s, sin_s)
            ks_ = rope_2d(ks_, cos_s, sin_s)
            k_ctx = cache_k[:, j].reshape(Bs, N_HEADS, Lc, 2, HEAD_DIM).astype(ks_.dtype)
            k_all = jnp.concatenate([k_ctx, ks_], axis=2)
            v_all = jnp.concatenate([cache_v[:, j].astype(vs.dtype), vs], axis=2)
            ys = diff_attention(qs, k_all, v_all, lam, lam_init, at_subln[j]) @ at_w_out[j]
        xp = xp + mp[2] * rmsnorm(yp, norm_w[i, 1])
        xs = xs + ms[2] * rmsnorm(ys, norm_w[i, 1])
        ff = (ffn_w_up[i], ffn_w_dw[i], ffn_b_dw[i], ffn_w_down[i])
        hp = modulate(rmsnorm(xp, norm_w[i, 2]), mp[3], mp[4])
        hs = modulate(rmsnorm(xs, norm_w[i, 2]), ms[3], ms[4])
        xp = xp + mp[5] * rmsnorm(conv_ffn(hp, *ff), norm_w[i, 3])
        xs = xs + ms[5] * rmsnorm(conv_ffn(hs, *ff), norm_w[i, 3])
    new_cache_k = jnp.stack(new_k, axis=1)
    new_cache_v = jnp.stack(new_v, axis=1)
    return (xp, xs, new_cache_k, new_cache_v)
```

```python
import math
import os
from contextlib import ExitStack
import numpy as np
import ml_dtypes
import concourse.bass as bass
import concourse.mybir as mybir
from concourse.bass_utils import run_bass_kernel_spmd

F32, BF16 = mybir.dt.float32, mybir.dt.bfloat16
AF = mybir.ActivationFunctionType
ALU = mybir.AluOpType
D = 1024
KC = 8
DFF = 2816
T = 2560
LS, LP = 2048, 256
EPS = 1e-6
CH = [(0, 512, 1, 0, 0), (512, 512, 0, 0, 0), (1024, 512, 0, 0, 0), (1536, 512, 0, 1, 0),
      (2048, 256, 1, 1, 1), (2304, 256, 1, 1, 1)]
BLKENG = {'pe': 'tensor', 'act': 'scalar', 'dve': 'vector', 'pool': 'gpsimd', 'sp': 'sync'}


class Sched:
    def __init__(self, nc, es):
        self.nc, self.es = nc, es
        self.E = ['pe', 'act', 'dve', 'pool', 'sp']
        self.q = {e: [] for e in self.E}
        self.sems = []
        self.cur = {}
        self.cnt = {}
        for e in ['pe', 'act', 'dve', 'pool']:
            self.cur[e] = self.newsem()
            self.cnt[e] = 0
        self.waited = {e: {} for e in self.E}
        self.lastw, self.readers = {}, {}
        self.dsem = {e: [self.newsem() for _ in range(8)] for e in ['sp', 'pool']}
        self.dval = {e: [0] * 8 for e in ['sp', 'pool']}
        self.dcnt = {'sp': 0, 'pool': 0}
        self.live = set()
        self.nops = 0

    def newsem(self):
        self.sems.append(self.es.enter_context(self.nc.semaphore(f"s{len(self.sems)}")))
        return len(self.sems) - 1

    def _wait(self, eng, tok):
        si, v, en = tok
        if en == eng and (eng == 'pe' or os.environ.get('K_NOSAME')):
            return
        if self.waited[eng].get(si, 0) >= v:
            return
        self.q[eng].append(('w', si, v))
        self.waited[eng][si] = v

    def op(self, eng, fn, r=(), w=(), dma=False, signal=True):
        self.nops += 1
        psr = [k_ for k_ in r if k_[:2] == 'ps' and k_[2:].isdigit()]
        if psr:
            r = [k_ for k_ in r if k_ not in psr]
            w = list(w) + psr
        deps = []
        for k in r:
            t = self.lastw.get(k)
            if t:
                deps.append(t)
        for k in w:
            t = self.lastw.get(k)
            if t:
                deps.append(t)
            deps.extend(self.readers.get(k, ()))
        for t in deps:
            self._wait(eng, t)
        if dma:
            i = self.dcnt[eng]
            slot = i % 8
            si = self.dsem[eng][slot]
            prev = self.dval[eng][slot]
            if prev > 0:
                self._wait(eng, (si, prev, 'dma'))
            self.dval[eng][slot] = prev + 16
            self.dcnt[eng] += 1
            tok = (si, prev + 16, 'dma')
            self.q[eng].append(('o', fn, si, 16))
        elif signal:
            if self.cnt[eng] >= 30000:
                self.cur[eng] = self.newsem()
                self.cnt[eng] = 0
            self.cnt[eng] += 1
            tok = (self.cur[eng], self.cnt[eng], eng)
            self.q[eng].append(('o', fn, self.cur[eng], 1))
        else:
            self.q[eng].append(('o', fn, None, 0))
            return None
        self.live.add(tok)
        for k in w:
            self.lastw[k] = tok
            self.readers[k] = []
        for k in r:
            self.readers.setdefault(k, []).append(tok)
        return tok

    def barrier(self):
        latest = {}
        for t in self.live:
            if latest.get(t[0], (0, 0, 0))[1] < t[1]:
                latest[t[0]] = t
        for e in self.E:
            for t in latest.values():
                if not (t[2] == e and e == 'pe'):
                    si, v, en = t
                    if self.waited[e].get(si, 0) < v:
                        self.q[e].append(('w', si, v))
                        self.waited[e][si] = v
        self.live = set(latest.values())
        self.lastw, self.readers = {}, {}

    def dma(self, q, out, in_, r=(), w=()):
        return self.op(q, lambda e: e.dma_start(out=out, in_=in_), r, w, dma=True)

    def mm(self, out, lhsT, rhs, start, stop, r=(), w=(), signal=True):
        return self.op('pe', lambda e: e.matmul(out, lhsT, rhs, start=start, stop=stop), r, w, signal=signal)

    def tr(self, out, in_, ident, r=(), w=()):
        return self.op('pe', lambda e: e.transpose(out, in_, ident), r, w)

    def act(self, out, in_, func, bias=0.0, scale=1.0, r=(), w=(), accum=None):
        if accum is None:
            return self.op('act', lambda e: e.activation(out=out, in_=in_, func=func, bias=bias, scale=scale), r, w)
        return self.op('act', lambda e: e.activation(out=out, in_=in_, func=func, bias=bias, scale=scale,
                                                     accum_out=accum), r, w)

    def ts(self, eng, out, in0, s1, s2, op0, op1=None, r=(), w=()):
        if op1 is None:
            return self.op(eng, lambda e: e.tensor_scalar(out=out, in0=in0, scalar1=s1, scalar2=None, op0=op0), r, w)
        return self.op(eng, lambda e: e.tensor_scalar(out=out, in0=in0, scalar1=s1, scalar2=s2, op0=op0, op1=op1), r, w)

    def tt(self, eng, out, in0, in1, op, r=(), w=()):
        return self.op(eng, lambda e: e.tensor_tensor(out=out, in0=in0, in1=in1, op=op), r, w)

    def stt(self, eng, out, in0, scalar, in1, op0, op1, r=(), w=()):
        return self.op(eng, lambda e: e.scalar_tensor_tensor(out=out, in0=in0, scalar=scalar, in1=in1,
                                                             op0=op0, op1=op1), r, w)

    def copy(self, eng, out, in_, r=(), w=()):
        if eng == 'act':
            return self.act(out, in_, AF.Identity, r=r, w=w)
        return self.op(eng, lambda e: e.tensor_copy(out=out, in_=in_), r, w)

    def recip(self, out, in_, r=(), w=()):
        return self.op('dve', lambda e: e.reciprocal(out=out, in_=in_), r, w)

    def memset(self, eng, ap, val, r=(), w=()):
        return self.op(eng, lambda e: e.memset(ap, val), r, w)


class K:
    pass


def build(prog, dbg=False):
    nc = bass.Bass("TRN2", target_bir_lowering=False)
    es = ExitStack()
    k = K()
    k.nc = nc

    def din(name, shape, dt=F32):
        return nc.dram_tensor(name, list(shape), dt, kind="ExternalInput").ap()

    def dout(name, shape, dt=F32):
        return nc.dram_tensor(name, list(shape), dt, kind="ExternalOutput").ap()

    def dscr(name, shape, dt=F32):
        return nc.dram_tensor(name, list(shape), dt, kind="Internal").ap()

    I = {}
    I['xin'] = din('xin', [T, D])
    I['condT'] = din('condT', [128, KC, 2])
    I['w_ada'] = din('w_ada', [4, D, 6 * D])
    I['b_adaT'] = din('b_adaT', [128, 4, 48])
    I['norm_wT'] = din('norm_wT', [128, 4, 4, 8])
    I['hy_w_in'] = din('hy_w_in', [2, D, 3 * D])
    I['hy_b_inT'] = din('hy_b_inT', [128, 2, 24])
    I['hy_w_shortT'] = din('hy_w_shortT', [128, 2, 3, 24])
    I['hy_b_shortT'] = din('hy_b_shortT', [128, 2, 24])
    I['hy_f_w1'] = din('hy_f_w1', [2, 33, 64])
    I['hy_fvecT'] = din('hy_fvecT', [2, 64, 3])
    I['hy_f_w2'] = din('hy_f_w2', [2, 64, 64])
    I['hy_f_w3'] = din('hy_f_w3', [2, 64, 2 * D])
    I['hy_d_biasT'] = din('hy_d_biasT', [128, 2, 8])
    I['hy_w_out'] = din('hy_w_out', [2, D, D])
    I['hy_b_outT'] = din('hy_b_outT', [128, 2, 8])
    I['at_w_qkv'] = din('at_w_qkv', [2, D, 3 * D])
    I['at_w_out'] = din('at_w_out', [2, D, D])
    I['lam_bc'] = din('lam_bc', [128, 2, 4, 64])
    I['subln_bc'] = din('subln_bc', [128, 2, 128])
    I['ffn_w_up'] = din('ffn_w_up', [4, D, 2 * DFF])
    I['ffn_w_dwT'] = din('ffn_w_dwT', [128, 4, 3, 44])
    I['ffn_b_dwT'] = din('ffn_b_dwT', [128, 4, 44])
    I['ffn_w_down'] = din('ffn_w_down', [4, DFF, D])
    I['cache_k'] = din('cache_k', [2, 8, 256, 128])
    I['cache_v'] = din('cache_v', [2, 8, 256, 128])
    I['ident'] = din('ident', [128, 128])
    I['msk'] = din('msk', [128, 2])
    I['rotm'] = din('rotm', [128, 128])
    I['ropec'] = din('ropec', [128, LS])
    I['ropes'] = din('ropes', [128, LS])
    for L, nm in ((LS, 's'), (LP, 'p')):
        I['FT' + nm] = din('FT' + nm, [L // 128, 128, 2 * L], BF16)
        I['IT' + nm] = din('IT' + nm, [L // 256, 128, 2 * (L // 128) * 256], BF16)
        I['posT' + nm] = din('posT' + nm, [33, L])
        I['tl' + nm] = din('tl' + nm, [128, L // 128])
    I['absd'] = din('absd', [128, D])
    O = {}
    O['yout'] = dout('yout', [T, D])
    O['nk'] = dout('nk', [2, 2, 8, 256, 128])
    O['nv'] = dout('nv', [2, 2, 8, 256, 128])
    X = dscr('X', [KC, 128, T])
    Gd = dscr('Gd', [KC, 128, T], BF16)
    Ad = dscr('Ad', [22, 128, T], BF16)
    X0d = dscr('X0d', [KC, 128, T])
    Zd = dscr('Zd', [KC, 128, T])
    ZTd = dscr('ZTd', [T, D], BF16)
    Yd = dscr('Yd', [KC, 128, T])

    S = Sched(nc, es)

    uid = [0]

    def sb(st, name, shape, dt=F32):
        uid[0] += 1
        return st.enter_context(nc.sbuf_tensor(f"t{uid[0]}_{name}", list(shape), dt))

    ps = [es.enter_context(nc.psum_tensor(f"ps{i}", [128, 512], F32)) for i in range(8)]
    PSK = [f"ps{i}" for i in range(8)]

    ident = sb(es, 'ident', [128, 128])
    ones_bf = sb(es, 'ones_bf', [128, 128], BF16)
    DER = sb(es, 'DER', [128, 4, 6, 8, 2])
    normw = sb(es, 'normw', [128, 4, 4, 8])
    xb = [sb(es, f'xb{i}', [128, KC, 512]) for i in range(2)]
    sq = sb(es, 'sq', [128, KC, 512], BF16)
    rstd = sb(es, 'rstd', [128, 512])
    tmp = [sb(es, f'tmp{i}', [128, 512]) for i in range(2)]
    ybuf = sb(es, 'ybuf', [128, KC, 512])

    S.dma('sp', ident[:], I['ident'][:, :], w=['ident'])
    S.memset('dve', ones_bf[:], 1.0, w=['ones'])
    S.dma('sp', normw[:], I['norm_wT'][:, :, :, :], w=['normw'])

    with ExitStack() as st:
        MOD = sb(st, 'MOD', [128, 4, 48, 2])
        condT = sb(st, 'condT', [128, KC, 2])
        scond = sb(st, 'scond', [128, KC, 2], BF16)
        badaT = sb(st, 'badaT', [128, 4, 48])
        wada = [sb(st, f'wada{i}', [128, KC, 512], BF16) for i in range(3)]
        S.dma('sp', condT[:], I['condT'][:, :, :], w=['condT'])
        S.dma('sp', badaT[:], I['b_adaT'][:, :, :], w=['badaT'])
        S.act(scond[:], condT[:], AF.Silu, r=['condT'], w=['scond'])
        blk = 0
        for i in range(4):
            for nb in range(12):
                wt = wada[blk % 3]
                wk = f'wada{blk % 3}'
                S.dma('pool', wt[:, :, :],
                      I['w_ada'][i, :, nb * 512:(nb + 1) * 512].rearrange("(k p) n -> p k n", p=128), w=[wk])
                for m in range(4):
                    fc = nb * 4 + m
                    pt = ps[fc % 2]
                    for kc in range(KC):
                        S.mm(pt[:, 0:2], wt[:, kc, m * 128:(m + 1) * 128], scond[:, kc, :], kc == 0, kc == KC - 1,
                             r=[wk, 'scond'], w=[PSK[fc % 2]], signal=(kc == KC - 1))
                    S.ts('dve', MOD[:, i, fc, :], pt[:, 0:2], badaT[:, i, fc:fc + 1], None, ALU.add,
                         r=[PSK[fc % 2], 'badaT'], w=['MOD'])
                blk += 1
        for i in range(4):
            for half in range(2):
                b = half * 24
                for kc in range(KC):
                    S.ts('dve', DER[:, i, half * 3 + 0, kc, :], MOD[:, i, b + 8 + kc, :], 1.0,
                         normw[:, i, half * 2, kc:kc + 1], ALU.add, ALU.mult, r=['MOD', 'normw'], w=['DER'])
                    S.copy('dve', DER[:, i, half * 3 + 1, kc, :], MOD[:, i, b + kc, :], r=['MOD'], w=['DER'])
                    S.ts('dve', DER[:, i, half * 3 + 2, kc, :], MOD[:, i, b + 16 + kc, :],
                         normw[:, i, half * 2 + 1, kc:kc + 1], None, ALU.mult, r=['MOD', 'normw'], w=['DER'])
        S.barrier()

    def xview(t0, n):
        return X[:, :, t0:t0 + n].rearrange("k p t -> p k t")

    def rms_rstd(src, n, rk, feat=1024.0):
        S.act(sq[:, :, :n], src, AF.Square, r=rk, w=['sq'])
        for kc in range(KC):
            S.mm(ps[7][:, :n], ones_bf[:], sq[:, kc, :n], kc == 0, kc == KC - 1, r=['sq', 'ones'], w=['ps7'],
                 signal=(kc == KC - 1))
        S.ts('dve', rstd[:, :n], ps[7][:, :n], 1.0 / feat, EPS, ALU.mult, ALU.add, r=['ps7'], w=['rstd'])
        S.act(rstd[:, :n], rstd[:, :n], AF.Sqrt, r=['rstd'], w=['rstd'])
        S.recip(rstd[:, :n], rstd[:, :n], r=['rstd'], w=['rstd'])

    def stage_in():
        for ci in range(5):
            t0 = ci * 512
            xt = xb[ci % 2]
            xk = f'xb{ci % 2}'
            with ExitStack() as st:
                pass
            for tt in range(4):
                S.dma('sp', ybuf[:, 2 * tt:2 * tt + 2, :].rearrange("p a b -> p (a b)"),
                      I['xin'][t0 + tt * 128:t0 + (tt + 1) * 128, :], w=[f'yb{tt}'])
            for tt in range(4):
                for kc in range(KC):
                    pi = (tt * KC + kc) % 4
                    src = ybuf[:, 2 * tt:2 * tt + 2, :].rearrange("p a b -> p (a b)")[:, kc * 128:(kc + 1) * 128]
                    S.tr(ps[pi][:, 0:128], src, ident[:], r=[f'yb{tt}', 'ident'], w=[PSK[pi]])
                    S.copy('dve' if kc % 2 else 'act', xt[:, kc, tt * 128:(tt + 1) * 128], ps[pi][:, 0:128],
                           r=[PSK[pi]], w=[xk])
            S.dma('sp', xview(t0, 512), xt[:, :, :], r=[xk], w=[f'X{t0}'])
        S.barrier()

    def stage_out():
        for ci in range(5):
            t0 = ci * 512
            xt = xb[ci % 2]
            xk = f'xb{ci % 2}'
            S.dma('sp', xt[:, :, :], xview(t0, 512), r=[f'X{t0}'], w=[xk])
            for tt in range(4):
                yv = ybuf[:, 2 * tt:2 * tt + 2, :].rearrange("p a b -> p (a b)")
                for kc in range(KC):
                    pi = (tt * KC + kc) % 4
                    S.tr(ps[pi][:, 0:128], xt[:, kc, tt * 128:(tt + 1) * 128], ident[:], r=[xk, 'ident'], w=[PSK[pi]])
                    S.copy('dve' if kc % 2 else 'act', yv[:, kc * 128:(kc + 1) * 128], ps[pi][:, 0:128],
                           r=[PSK[pi]], w=[f'yb{tt}'])
                tk = S.dma('sp', O['yout'][t0 + tt * 128:t0 + (tt + 1) * 128, :], yv, r=[f'yb{tt}'], w=[f'yo{ci}_{tt}'])
        S.barrier()

    def norm_to_hall(st, i, half):
        h_all = sb(st, 'h_all', [128, KC, T], BF16)
        for ci, (t0, n, lz, rz, cond) in enumerate(CH):
            xt = xb[ci % 2]
            xk = f'xb{ci % 2}'
            S.dma('sp', xt[:, :, :n], xview(t0, n), r=[f'X{t0}'], w=[xk])
            rms_rstd(xt[:, :, :n], n, [xk])
            for kc in range(KC):
                tp = tmp[kc % 2]
                S.tt('dve', tp[:, :n], xt[:, kc, :n], rstd[:, :n], ALU.mult, r=[xk, 'rstd'], w=[f'tmp{kc % 2}'])
                S.act(h_all[:, kc, t0:t0 + n], tp[:, :n], AF.Identity,
                      bias=DER[:, i, half * 3 + 1, kc, cond:cond + 1], scale=DER[:, i, half * 3 + 0, kc, cond:cond + 1],
                      r=[f'tmp{kc % 2}', 'DER'], w=[f'h{t0}'])
        return h_all

    def hkeys(t0, n):
        return [f'h{c[0]}' for c in CH if c[0] + c[1] >= t0 - 1 and c[0] <= t0 + n]

    def proj_out(i, half, src, nk, Wd, biasT, wres=None):
        with ExitStack() as st:
            if wres is None:
                wres = sb(st, 'wres', [128, nk, D], BF16)
                S.dma('pool', wres[:, :, :], Wd.rearrange("(k p) n -> p k n", p=128), w=['wres'])
            gcs = [sb(st, f'gc{b_}', [128, nk, 512], BF16) for b_ in range(2)]

            def ld_(c_):
                t0_, n_ = CH[c_][0], CH[c_][1]
                S.dma('sp', gcs[c_ % 2][:, :, :n_], src[:, :, t0_:t0_ + n_].rearrange("k p t -> p k t"),
                      r=[f'G{t0_}'], w=[f'gc{c_ % 2}'])
                S.dma('sp', xb[c_ % 2][:, :, :n_], xview(t0_, n_), r=[f'X{t0_}'], w=[f'xb{c_ % 2}'])
            ld_(0)
            for ci, (t0, n, lz, rz, cond) in enumerate(CH):
                xt = xb[ci % 2]
                xk = f'xb{ci % 2}'
                gc = gcs[ci % 2]
                gck = f'gc{ci % 2}'
                if ci + 1 < len(CH):
                    ld_(ci + 1)
                for oc in range(KC):
                    pt = ps[oc % 4]
                    for kc in range(nk):
                        S.mm(pt[:, :n], wres[:, kc, oc * 128:(oc + 1) * 128], gc[:, kc, :n], kc == 0, kc == nk - 1,
                             r=['wres', gck], w=[PSK[oc % 4]], signal=(kc == nk - 1))
                    bias = 0.0 if biasT is None else biasT[:, oc:oc + 1]
                    S.act(ybuf[:, oc, :n], pt[:, :n], AF.Identity, bias=bias, r=[PSK[oc % 4], 'bvec'], w=['ybuf'])
                rms_rstd(ybuf[:, :, :n], n, ['ybuf'])
                for oc in range(KC):
                    tp = tmp[oc % 2]
                    S.tt('dve', tp[:, :n], ybuf[:, oc, :n], rstd[:, :n], ALU.mult, r=['ybuf', 'rstd'],
                         w=[f'tmp{oc % 2}'])
                    S.stt('dve', xt[:, oc, :n], tp[:, :n], DER[:, i, half * 3 + 2, oc, cond:cond + 1], xt[:, oc, :n],
                          ALU.mult, ALU.add, r=[f'tmp{oc % 2}', 'DER', xk], w=[xk])
                S.dma('sp', xview(t0, n), xt[:, :, :n], r=[xk], w=[f'X{t0}'])
            S.barrier()

    def conv3(psm, psh, n, lz, rz, b_in, w0, w1, w2, b_out, ub, out, pk, hk, ubk, outk):
        bi = 0.0 if b_in is None else b_in
        S.act(ub[:, 1:n + 1], psm, AF.Identity, bias=bi, r=[pk, 'cvec'], w=[ubk])
        if lz:
            S.memset('pool', ub[:, 0:1], 0.0, w=[ubk])
        else:
            S.act(ub[:, 0:1], psh[:, 0:1], AF.Identity, bias=bi, r=[hk, 'cvec'], w=[ubk])
        if rz:
            S.memset('pool', ub[:, n + 1:n + 2], 0.0, w=[ubk])
        else:
            S.act(ub[:, n + 1:n + 2], psh[:, 1:2], AF.Identity, bias=bi, r=[hk, 'cvec'], w=[ubk])
        S.ts('dve', out[:, :n], ub[:, 1:n + 1], w1, b_out, ALU.mult, ALU.add, r=[ubk, 'cvec'], w=[outk])
        S.stt('dve', out[:, :n], ub[:, 0:n], w0, out[:, :n], ALU.mult, ALU.add, r=[ubk, 'cvec', outk], w=[outk])
        S.stt('dve', out[:, :n], ub[:, 2:n + 2], w2, out[:, :n], ALU.mult, ALU.add, r=[ubk, 'cvec', outk], w=[outk])

    def conv3f(psm, psh, n, lz, rz, w0, w1, w2, b_out, out, pk, hk, outk):
        S.act(out[:, :n], psm, AF.Identity, bias=b_out, scale=w1, r=[pk, 'cvec', 'cvec2'], w=[outk])
        S.stt('dve', out[:, 1:n], psm[:, 0:n - 1], w0, out[:, 1:n], ALU.mult, ALU.add, r=[pk, 'cvec', outk], w=[outk])
        S.stt('dve', out[:, 0:n - 1], psm[:, 1:n], w2, out[:, 0:n - 1], ALU.mult, ALU.add, r=[pk, 'cvec', outk],
              w=[outk])
        if not lz:
            S.stt('dve', out[:, 0:1], psh[:, 0:1], w0, out[:, 0:1], ALU.mult, ALU.add, r=[hk, 'cvec', outk], w=[outk])
        if not rz:
            S.stt('dve', out[:, n - 1:n], psh[:, 1:2], w2, out[:, n - 1:n], ALU.mult, ALU.add, r=[hk, 'cvec', outk],
                  w=[outk])

    def lin_chunk(h_all, wt, wk, col0, t0, n, lz, rz, pm, ph):
        hk = hkeys(t0, n)
        for kc in range(KC):
            S.mm(ps[pm][:, :n], wt[:, kc, col0:col0 + 128], h_all[:, kc, t0:t0 + n], kc == 0, kc == KC - 1,
                 r=[wk] + hk, w=[PSK[pm]], signal=(kc == KC - 1))
        if not (lz and rz):
            a = t0 - 1 if not lz else t0
            b = t0 + n if not rz else t0 + n - 1
            for kc in range(KC):
                S.mm(ps[ph][:, 0:2], wt[:, kc, col0:col0 + 128], h_all[:, kc, a:b + 1:b - a], kc == 0, kc == KC - 1,
                     r=[wk] + hk, w=[PSK[ph]], signal=(kc == KC - 1))

    def ffn(i):
      with ExitStack() as st0:
        wres_ = sb(st0, 'wresf', [128, 22, D], BF16)
        with ExitStack() as st:
            h_all = norm_to_hall(st, i, 1)
            wdw = sb(st, 'wdw', [128, 3, 44])
            bdw = sb(st, 'bdw', [128, 44])
            S.dma('sp', wdw[:], I['ffn_w_dwT'][:, i, :, :], w=['cvec'])
            S.dma('sp', bdw[:], I['ffn_b_dwT'][:, i, :], w=['cvec'])
            wg = [sb(st, f'wg{b}', [128, KC, 256], BF16) for b in range(2)]
            wv = [sb(st, f'wv{b}', [128, KC, 256], BF16) for b in range(2)]
            cgs = [sb(st, f'cg{b_}', [128, 512]) for b_ in range(2)]
            cvs = [sb(st, f'cv{b_}', [128, 512]) for b_ in range(2)]
            sgs = [sb(st, f'sg{b_}', [128, 512]) for b_ in range(2)]
            at = [sb(st, f'at{b}', [128, 512], BF16) for b in range(2)]
            Wup = I['ffn_w_up']
            it = 0
            for jb in range(11):
                b = jb % 2
                S.dma('pool', wg[b][:, :, :],
                      Wup[i, :, jb * 256:(jb + 1) * 256].rearrange("(k p) n -> p k n", p=128), w=[f'wg{b}'])
                S.dma('pool', wv[b][:, :, :],
                      Wup[i, :, DFF + jb * 256:DFF + (jb + 1) * 256].rearrange("(k p) n -> p k n", p=128),
                      w=[f'wv{b}'])
                if jb == 2:
                    S.dma('pool', wres_[:, :, :], I['ffn_w_down'][i].rearrange("(k p) n -> p k n", p=128),
                          w=['wres'])
                for jj in range(2):
                    j = jb * 2 + jj
                    for ci, (t0, n, lz, rz, cond) in enumerate(CH):
                        p0 = 4 * (it % 2)
                        lin_chunk(h_all, wg[b], f'wg{b}', jj * 128, t0, n, lz, rz, p0, p0 + 1)
                        lin_chunk(h_all, wv[b], f'wv{b}', jj * 128, t0, n, lz, rz, p0 + 2, p0 + 3)
                        cg, cv, sg = cgs[it % 2], cvs[it % 2], sgs[it % 2]
                        cgk, cvk, sgk = f'cg{it % 2}', f'cv{it % 2}', f'sg{it % 2}'
                        conv3f(ps[p0][:, :n], ps[p0 + 1], n, lz, rz, wdw[:, 0, j:j + 1], wdw[:, 1, j:j + 1],
                               wdw[:, 2, j:j + 1], bdw[:, j:j + 1], cg, PSK[p0], PSK[p0 + 1], cgk)
                        conv3f(ps[p0 + 2][:, :n], ps[p0 + 3], n, lz, rz, wdw[:, 0, 22 + j:23 + j],
                               wdw[:, 1, 22 + j:23 + j], wdw[:, 2, 22 + j:23 + j], bdw[:, 22 + j:23 + j], cv,
                               PSK[p0 + 2], PSK[p0 + 3], cvk)
                        S.act(sg[:, :n], cg[:, :n], AF.Silu, r=[cgk], w=[sgk])
                        a = at[it % 2]
                        S.tt('pool', a[:, :n], sg[:, :n], cv[:, :n], ALU.mult, r=[sgk, cvk], w=[f'at{it % 2}'])
                        S.dma('sp', Ad[j, :, t0:t0 + n], a[:, :n], r=[f'at{it % 2}'], w=[f'A{j}_{t0}'])
                        it += 1
            S.barrier()
        proj_out(i, 1, Ad, 22, I['ffn_w_down'][i], None, wres=wres_)

    def attn(i):
        j = i // 2
        lam_init = 0.8 - 0.6 * math.exp(-0.3 * i)
        with ExitStack() as st:
            h_all = norm_to_hall(st, i, 0)
            lams = sb(st, 'lams', [128, 4])
            subw = sb(st, 'subw', [128, 128])
            rcts = [sb(st, f'rc{b_}', [128, 512]) for b_ in range(2)]
            rsts = [sb(st, f'rs{b_}', [128, 512]) for b_ in range(2)]
            S.dma('sp', subw[:], I['subln_bc'][:, j, :], w=['subw'])
            wrot = [sb(st, f'wrot{b}', [128, KC, 128], BF16) for b in range(2)]
            qT = sb(st, 'qT', [128, T], BF16)
            kT = sb(st, 'kT', [128, 256 + T], BF16)
            kT2 = sb(st, 'kT2', [128, 256 + T], BF16)
            msk = sb(st, 'msk', [128, 2])
            S.dma('sp', msk[:], I['msk'][:, :], w=['msk'])
            Vh = sb(st, 'Vh', [128, 22, 132], BF16)
            qf = sb(st, 'qf', [128, 512])
            t1 = sb(st, 't1', [128, 512])
            t2 = sb(st, 't2', [128, 512])
            qfb = sb(st, 'qfb', [128, 512])
            t1b = sb(st, 't1b', [128, 512])
            t2b = sb(st, 't2b', [128, 512])
            Es = [[sb(st, f'E{g_}_{s}', [128, 18, 512], BF16) for s in range(2)] for g_ in range(1)]
            o1g = sb(st, 'o1g', [128, 4, 132])
            o2g = sb(st, 'o2g', [128, 4, 132])
            rr = sb(st, 'rr', [128, 4, 4])
            gT = sb(st, 'gT', [128, 512], BF16)
            ong = qf[:, :].rearrange("p (a b) -> p a b", b=128)
            sqg = t1[:, :].rearrange("p (a b) -> p a b", b=128)
            kvo = t1[:, 0:256].rearrange("p (a b) -> p a b", b=128)
            kvo2 = t1[:, 256:512].rearrange("p (a b) -> p a b", b=128)
            ck = kvo2
            lamt = qf[:, 0:256].rearrange("p (a b) -> p a b", b=64)
            lamw = qf[:, 256:384].rearrange("p (a b) -> p a b", b=64)
            vf = t2
            S.dma('sp', lamt[:], I['lam_bc'][:, j, :, :], w=['qf'])
            S.tt('dve', lamw[:, 0, :], lamt[:, 0, :], lamt[:, 1, :], ALU.mult, r=['qf'], w=['qf'])
            S.tt('dve', lamw[:, 1, :], lamt[:, 2, :], lamt[:, 3, :], ALU.mult, r=['qf'], w=['qf'])
            w_ = 64
            while w_ > 1:
                w_ //= 2
                S.tt('dve', lamw[:, :, 0:w_], lamw[:, :, 0:w_], lamw[:, :, w_:2 * w_], ALU.add, r=['qf'], w=['qf'])
            S.copy('dve', lams[:, 0:1], lamw[:, 0, 0:1], r=['qf'], w=['lams'])
            S.copy('dve', lams[:, 1:2], lamw[:, 1, 0:1], r=['qf'], w=['lams'])
            S.act(lams[:, 0:2], lams[:, 0:2], AF.Exp, r=['lams'], w=['lams'])
            S.tt('dve', lams[:, 2:3], lams[:, 0:1], lams[:, 1:2], ALU.subtract, r=['lams'], w=['lams'])
            S.ts('dve', lams[:, 2:3], lams[:, 2:3], lam_init, None, ALU.add, r=['lams'], w=['lams'])

            wqs = [sb(st, f'wq{b_}', [128, KC, 128], BF16) for b_ in range(2)]
            wks = [sb(st, f'wk{b_}', [128, KC, 128], BF16) for b_ in range(2)]
            wvs = [sb(st, f'wv{b_}', [128, KC, 128], BF16) for b_ in range(2)]
            S.memset('dve', Vh[:, :, 128:129], 1.0, w=['Vh'])
            Wqkv = I['at_w_qkv']
            import os
            SK = set(os.environ.get('ATT_SKIP', '').split(','))
            NH = int(os.environ.get('ATT_HEADS', '8'))
            def ldw(hd_):
                b_ = hd_ % 2
                for q_, (wl, nm_) in enumerate(((wqs, 'wq'), (wks, 'wk'), (wvs, 'wv'))):
                    S.dma('pool', wl[b_][:, :, :],
                          Wqkv[j, :, q_ * D + hd_ * 128:q_ * D + (hd_ + 1) * 128].rearrange("(k p) n -> p k n", p=128),
                          w=[f'{nm_}{b_}'])
            ldw(0)
            for hd in range(NH):
                if hd + 1 < NH:
                    ldw(hd + 1)
                wq, wk_, wv = wqs[hd % 2], wks[hd % 2], wvs[hd % 2]
                WQK, WKK, WVK = f'wq{hd % 2}', f'wk{hd % 2}', f'wv{hd % 2}'
                for wi_, wsrc, wkey_ in ((0, wq, WQK), (1, wk_, WKK)):
                    for s2 in range(2):
                        for a2 in range(2):
                            b0 = s2 * 64 + a2 * 32
                            S.ts('dve', wrot[wi_][:, :, b0:b0 + 16], wsrc[:, :, b0 + 16:b0 + 32], -1.0, None, ALU.mult,
                                 r=[wkey_], w=['wrot'])
                            S.copy('dve', wrot[wi_][:, :, b0 + 16:b0 + 32], wsrc[:, :, b0:b0 + 16], r=[wkey_], w=['wrot'])
                if 'cache' in SK:
                    pass
                for a_ in range(2):
                    S.dma('pool', ck[:, a_, :], I['cache_k'][j, hd, a_ * 128:(a_ + 1) * 128, :], w=['t1'])
                for a_ in range(2):
                    S.dma('pool', Vh[:, a_, 0:128], I['cache_v'][j, hd, a_ * 128:(a_ + 1) * 128, :], w=['Vh'])
                for a in range(2):
                    S.tr(ps[4][:, a * 128:(a + 1) * 128], ck[:, a, :], ident[:], r=['t1', 'ident'], w=['ps4'])
                S.ts('dve', kT[:, 0:256], ps[4][:, 0:256], msk[:, 0:1], None, ALU.mult, r=['ps4', 'msk'], w=['kT'])
                S.ts('dve', kT2[:, 0:256], ps[4][:, 0:256], msk[:, 1:2], None, ALU.mult, r=['ps4', 'msk'], w=['kT'])
                for ci, (t0, n, lz, rz, cond) in enumerate(CH):
                    if cond == 0 and 'rope' not in SK:
                        rc, rs = rcts[ci % 2], rsts[ci % 2]
                        RK = f'rope{ci % 2}'
                        S.dma('pool', rc[:, :n], I['ropec'][:, t0:t0 + n], w=[RK + 'c'])
                        S.dma('pool', rs[:, :n], I['ropes'][:, t0:t0 + n], w=[RK + 's'])
                    for which, wt, wkey, dst, doff in ((0, wq, WQK, qT, 0), (1, wk_, WKK, None, 256)):
                        qf_, t1_, t2_ = (qf, t1, t2) if which == 0 else (qfb, t1b, t2b)
                        QK_, T1K, T2K = ('qf', 't1', 't2') if which == 0 else ('qfb', 't1b', 't2b')
                        prp = 2 + which
                        for kc in range(KC):
                            S.mm(ps[which][:, :n], wt[:, kc, :], h_all[:, kc, t0:t0 + n], kc == 0, kc == KC - 1,
                                 r=[wkey, f'h{t0}'], w=[PSK[which]], signal=(kc == KC - 1))
                        dk = 'qT' if which == 0 else 'kT'
                        if cond == 0 and 'rope' not in SK:
                            S.copy('act', qf_[:, :n], ps[which][:, :n], r=[PSK[which]], w=[QK_])
                            for kc in range(KC):
                                S.mm(ps[prp][:, :n], wrot[which][:, kc, :], h_all[:, kc, t0:t0 + n], kc == 0,
                                     kc == KC - 1, r=['wrot', f'h{t0}'], w=[PSK[prp]], signal=(kc == KC - 1))
                            S.tt('dve', t1_[:, :n], qf_[:, :n], rc[:, :n], ALU.mult, r=[QK_, RK + 'c'], w=[T1K])
                            S.tt('dve', t2_[:, :n], ps[prp][:, :n], rs[:, :n], ALU.mult, r=[PSK[prp], RK + 's'],
                                 w=[T2K])
                            if which == 0:
                                S.tt('dve', qT[:, t0:t0 + n], t1_[:, :n], t2_[:, :n], ALU.add, r=[T1K, T2K], w=[dk])
                            else:
                                S.tt('dve', t1_[:, :n], t1_[:, :n], t2_[:, :n], ALU.add, r=[T1K, T2K], w=[T1K])
                                S.ts('dve', kT[:, doff + t0:doff + t0 + n], t1_[:, :n], msk[:, 0:1], None, ALU.mult,
                                     r=[T1K, 'msk'], w=[dk])
                                S.ts('dve', kT2[:, doff + t0:doff + t0 + n], t1_[:, :n], msk[:, 1:2], None, ALU.mult,
                                     r=[T1K, 'msk'], w=[dk])
                        else:
                            if which == 0:
                                S.copy('act', qT[:, t0:t0 + n], ps[which][:, :n], r=[PSK[which]], w=[dk])
                            else:
                                S.ts('dve', kT[:, doff + t0:doff + t0 + n], ps[which][:, :n], msk[:, 0:1], None,
                                     ALU.mult, r=[PSK[which], 'msk'], w=[dk])
                                S.ts('dve', kT2[:, doff + t0:doff + t0 + n], ps[which][:, :n], msk[:, 1:2], None,
                                     ALU.mult, r=[PSK[which], 'msk'], w=[dk])
                for ci, (t0, n, lz, rz, cond) in enumerate(CH):
                    nt_ = n // 128
                    tt0 = t0 // 128
                    for kc in range(KC):
                        S.mm(ps[3][:, :n], wv[:, kc, :], h_all[:, kc, t0:t0 + n], kc == 0, kc == KC - 1,
                             r=[WVK, f'h{t0}'], w=['ps3'], signal=(kc == KC - 1))
                    S.copy('act', vf[:, :n], ps[3][:, :n], r=['ps3'], w=['t2'])
                    V2 = int(os.environ.get('V2_STOP', '9'))
                    if V2 < 2:
                        continue
                    for a_ in range(nt_):
                        S.tr(ps[5][:, a_ * 128:(a_ + 1) * 128], vf[:, a_ * 128:(a_ + 1) * 128], ident[:],
                             r=['t2', 'ident'], w=['ps5'])
                    if V2 < 3:
                        continue
                    S.copy('act', Vh[:, 2 + tt0:2 + tt0 + nt_, 0:128],
                           ps[5][:, 0:n].rearrange("p (a b) -> p a b", b=128), r=['ps5'], w=['Vh'])
                    if V2 < 4:
                        continue
                    if cond == 1:
                        sq_ = (t0 - 2048) // 256
                        S.copy('dve', kvo[:, :, :], ps[5][:, 0:256].rearrange("p (a b) -> p a b", b=128),
                               r=['ps5'], w=['t1'])
                        if V2 >= 5:
                            S.dma('sp', O['nv'][sq_, j, hd, :, :].rearrange("(a p) d -> p a d", p=128), kvo[:, :, :],
                                  r=['t1'], w=[f'nv{hd}_{sq_}'])
                        if V2 < 6:
                            continue
                        for kc in range(KC):
                            S.mm(ps[3][:, :n], wk_[:, kc, :], h_all[:, kc, t0:t0 + n], kc == 0, kc == KC - 1,
                                 r=[WKK, f'h{t0}'], w=['ps3'], signal=(kc == KC - 1))
                        S.copy('act', vf[:, :n], ps[3][:, :n], r=['ps3'], w=['t2'])
                        if V2 < 7:
                            continue
                        for a_ in range(nt_):
                            S.tr(ps[5][:, a_ * 128:(a_ + 1) * 128], vf[:, a_ * 128:(a_ + 1) * 128], ident[:],
                                 r=['t2', 'ident'], w=['ps5'])
                        if V2 < 8:
                            continue
                        S.copy('dve', kvo2[:, :, :], ps[5][:, 0:256].rearrange("p (a b) -> p a b", b=128),
                               r=['ps5'], w=['t1'])
                        if V2 < 9:
                            continue
                        S.dma('sp', O['nk'][sq_, j, hd, :, :].rearrange("(a p) d -> p a d", p=128), kvo2[:, :, :],
                              r=['t1'], w=[f'nk{hd}_{sq_}'])
                groups = [(qc * 512, 512, list(range(18)), 0) for qc in range(4)]
                groups += [(2048, 256, [18, 19], 1), (2304, 256, [20, 21], 1)]
                import os
                LV = int(os.environ.get('ATT_LEVEL', '9'))
                if LV < 2:
                    groups = []
                for gi_, (q0, nq, ktl, isp) in enumerate(groups):
                    E = Es[0]
                    EK = ['E0_0', 'E0_1']
                    for ki, kt in enumerate(ktl):
                        kcol = kt * 128
                        for s_ in range(2):
                            pp = ps[s_ * 2 + (ki % 2)]
                            pk = PSK[s_ * 2 + (ki % 2)]
                            ksrc = kT if s_ == 0 else kT2
                            S.mm(pp[:, :nq], ksrc[:, kcol:kcol + 128], qT[:, q0:q0 + nq], True, True,
                                 r=['kT', 'qT'], w=[pk])
                            S.act(E[s_][:, ki, :nq], pp[:, :nq], AF.Exp, scale=0.125, r=[pk], w=[EK[s_]])
                    nqt = nq // 128 if LV >= 3 else 0
                    for qt in range(nqt):
                        for s_, ot, ok_ in ((0, o1g, 'o1'), (1, o2g, 'o2')):
                            pp = ps[4 + s_]
                            for ki, kt in enumerate(ktl):
                                S.mm(pp[:, 0:129], E[s_][:, ki, qt * 128:(qt + 1) * 128], Vh[:, kt, 0:129], ki == 0,
                                     ki == len(ktl) - 1, r=[EK[s_], 'Vh'], w=[PSK[4 + s_]],
                                     signal=(ki == len(ktl) - 1))
                            S.copy('act' if s_ == 0 else 'dve', ot[:, qt, 0:129], pp[:, 0:129], r=[PSK[4 + s_]],
                                   w=[ok_])
                    if nqt:
                        S.recip(rr[:, 0, 0:nqt], o1g[:, 0:nqt, 128], r=['o1'], w=['rr'])
                        S.recip(rr[:, 1, 0:nqt], o2g[:, 0:nqt, 128], r=['o2'], w=['rr'])
                        S.ts('dve', rr[:, 1, 0:nqt], rr[:, 1, 0:nqt], lams[:, 2:3], None, ALU.mult, r=['rr', 'lams'],
                             w=['rr'])
                        for qt in range(nqt):
                            S.ts('dve', o2g[:, qt, 0:128], o2g[:, qt, 0:128], rr[:, 1, qt:qt + 1], None, ALU.mult,
                                 r=['o2', 'rr'], w=['o2'])
                            S.stt('dve', o1g[:, qt, 0:128], o1g[:, qt, 0:128], rr[:, 0, qt:qt + 1], o2g[:, qt, 0:128],
                                  ALU.mult, ALU.subtract, r=['o1', 'o2', 'rr'], w=['o1'])
                        S.act(sqg[:, 0:nqt, :], o1g[:, 0:nqt, 0:128], AF.Square, r=['o1'], w=['t1'])
                        w_ = 128
                        while w_ > 1:
                            w_ //= 2
                            S.tt('dve', sqg[:, 0:nqt, 0:w_], sqg[:, 0:nqt, 0:w_], sqg[:, 0:nqt, w_:2 * w_], ALU.add,
                                 r=['t1'], w=['t1'])
                        S.ts('dve', rr[:, 2, 0:nqt], sqg[:, 0:nqt, 0], 1.0 / 128, EPS, ALU.mult, ALU.add, r=['t1'],
                             w=['rr2'])
                        S.act(rr[:, 2, 0:nqt], rr[:, 2, 0:nqt], AF.Sqrt, r=['rr2'], w=['rr2'])
                        S.recip(rr[:, 3, 0:nqt], rr[:, 2, 0:nqt], r=['rr2'], w=['rr2'])
                        S.ts('dve', rr[:, 3, 0:nqt], rr[:, 3, 0:nqt], 1.0 - lam_init, None, ALU.mult, r=['rr2'],
                             w=['rr2'])
                        for qt in range(nqt):
                            S.stt('dve', ong[:, qt, :], o1g[:, qt, 0:128], rr[:, 3, qt:qt + 1], subw[:], ALU.mult,
                                  ALU.mult, r=['o1', 'rr2', 'subw'], w=['qf'])
                            S.tr(ps[6][:, qt * 128:(qt + 1) * 128], ong[:, qt, :], ident[:], r=['qf', 'ident'],
                                 w=['ps6'])
                        S.copy('act', gT[:, 0:nq], ps[6][:, 0:nq], r=['ps6'], w=['gT'])
                    S.dma('sp', Gd[hd, :, q0:q0 + nq], gT[:, :nq], r=['gT'], w=[f'G{hd}_{q0}'])
            S.barrier()
        if 'proj' not in os.environ.get('ATT_SKIP', ''):
            proj_out(i, 0, Gd, KC, I['at_w_out'][j], None)

    def sin_rr(dst, src_ps, n, fr, fb, tmpt, keys_r, key_w):
        S.act(tmpt[:, :n], src_ps, AF.Identity, bias=fb, scale=fr, r=keys_r, w=['sintmp'])
        for _ in range(2):
            S.op('dve', lambda e: e.tensor_single_scalar(out=tmpt[:, 512:512 + n], in_=tmpt[:, :n], scalar=-math.pi,
                                                         op=ALU.is_lt), r=['sintmp'], w=['sinm'])
            S.stt('dve', tmpt[:, :n], tmpt[:, 512:512 + n], 2 * math.pi, tmpt[:, :n], ALU.mult, ALU.add,
                  r=['sinm', 'sintmp'], w=['sintmp'])
            S.op('dve', lambda e: e.tensor_single_scalar(out=tmpt[:, 512:512 + n], in_=tmpt[:, :n], scalar=math.pi,
                                                         op=ALU.is_gt), r=['sintmp'], w=['sinm'])
            S.stt('dve', tmpt[:, :n], tmpt[:, 512:512 + n], -2 * math.pi, tmpt[:, :n], ALU.mult, ALU.add,
                  r=['sinm', 'sintmp'], w=['sintmp'])
        S.act(dst, tmpt[:, :n], AF.Sin, r=['sintmp'], w=[key_w])

    def hyena(i):
        j = i // 2
        with ExitStack() as st:
            h_all = norm_to_hall(st, i, 0)
            bin_ = sb(st, 'bin', [128, 24])
            wsh = sb(st, 'wsh', [128, 3, 24])
            bsh = sb(st, 'bsh', [128, 24])
            S.dma('sp', bin_[:], I['hy_b_inT'][:, j, :], w=['cvec'])
            S.dma('sp', wsh[:], I['hy_w_shortT'][:, j, :, :], w=['cvec'])
            S.dma('sp', bsh[:], I['hy_b_shortT'][:, j, :], w=['cvec'])
            bsum = sb(st, 'bsum', [128, 24])
            bc0 = sb(st, 'bc0', [128, 24])
            bc2 = sb(st, 'bc2', [128, 24])
            S.tt('dve', bsum[:], wsh[:, 0, :], wsh[:, 1, :], ALU.add, r=['cvec'], w=['cvec2'])
            S.tt('dve', bsum[:], bsum[:], wsh[:, 2, :], ALU.add, r=['cvec', 'cvec2'], w=['cvec2'])
            S.tt('dve', bsum[:], bsum[:], bin_[:], ALU.mult, r=['cvec', 'cvec2'], w=['cvec2'])
            S.tt('dve', bsum[:], bsum[:], bsh[:], ALU.add, r=['cvec', 'cvec2'], w=['cvec2'])
            S.tt('dve', bc0[:], bin_[:], wsh[:, 0, :], ALU.mult, r=['cvec'], w=['cvec3'])
            S.ts('dve', bc0[:], bc0[:], -1.0, None, ALU.mult, r=['cvec3'], w=['cvec3'])
            S.tt('dve', bc2[:], bin_[:], wsh[:, 2, :], ALU.mult, r=['cvec'], w=['cvec4'])
            S.ts('dve', bc2[:], bc2[:], -1.0, None, ALU.mult, r=['cvec4'], w=['cvec4'])
            w3s = [[sb(st, f'w3_{b}_{d_}', [128, KC, 128], BF16) for b in range(3)] for d_ in range(2)]
            ub = [sb(st, f'ub{b}', [128, 514]) for b in range(3)]
            cc_ = [sb(st, f'cc{b}', [128, 512]) for b in range(3)]
            zts = [sb(st, f'zt{b_}', [128, 512]) for b_ in range(2)]
            zit = [0]
            pend = [None]
            zTt = sb(st, 'zTt', [128, 4, 128], BF16)
            Win = I['hy_w_in']
            def ldw3(cc_):
                for b in range(3):
                    S.dma('pool', w3s[cc_ % 2][b][:, :, :],
                          Win[j, :, b * D + cc_ * 128:b * D + (cc_ + 1) * 128].rearrange("(k p) n -> p k n", p=128),
                          w=[f'w3_{b}_{cc_ % 2}'])
            ldw3(0)
            for cc in range(8):
                if cc + 1 < 8:
                    ldw3(cc + 1)
                w3 = w3s[cc % 2]
                for ci, (t0, n, lz, rz, cond) in enumerate(CH):
                    for b in range(3):
                        lin_chunk(h_all, w3[b], f'w3_{b}_{cc % 2}', 0, t0, n, lz, rz, 2 * b, 2 * b + 1)
                        col = b * 8 + cc
                        conv3f(ps[2 * b][:, :n], ps[2 * b + 1], n, lz, rz,
                               wsh[:, 0, col:col + 1], wsh[:, 1, col:col + 1], wsh[:, 2, col:col + 1],
                               bsum[:, col:col + 1], cc_[b], PSK[2 * b], PSK[2 * b + 1], f'cc{b}')
                        if lz:
                            S.ts('dve', cc_[b][:, 0:1], cc_[b][:, 0:1], bc0[:, col:col + 1], None, ALU.add,
                                 r=[f'cc{b}', 'cvec3'], w=[f'cc{b}'])
                        if rz:
                            S.ts('dve', cc_[b][:, n - 1:n], cc_[b][:, n - 1:n], bc2[:, col:col + 1], None, ALU.add,
                                 r=[f'cc{b}', 'cvec4'], w=[f'cc{b}'])
                    S.dma('sp', X0d[cc, :, t0:t0 + n], cc_[0][:, :n], r=['cc0'], w=[f'X0{cc}_{t0}'])
                    zb_ = zit[0] % 2
                    zit[0] += 1
                    zt = zts[zb_]
                    S.tt('dve', zt[:, :n], cc_[1][:, :n], cc_[2][:, :n], ALU.mult, r=['cc1', 'cc2'], w=[f'zt{zb_}'])
                    S.dma('sp', Zd[cc, :, t0:t0 + n], zt[:, :n], r=[f'zt{zb_}'], w=[f'Z{cc}_{t0}'])
                    if pend[0] is not None:
                        pend[0]()

                    def mk_tr(zb_=zb_, n=n, t0=t0, cc=cc):
                        def f():
                            zt_ = zts[zb_]
                            for tt in range(n // 128):
                                S.tr(ps[6][:, tt * 128:(tt + 1) * 128], zt_[:, tt * 128:(tt + 1) * 128], ident[:],
                                     r=[f'zt{zb_}', 'ident'], w=['ps6'])
                            S.copy('act', zTt[:, 0:n // 128, :], ps[6][:, 0:n].rearrange("p (a b) -> p a b", b=128),
                                   r=['ps6'], w=['zTt'])
                            S.dma('sp', ZTd[t0:t0 + n, cc * 128:(cc + 1) * 128].rearrange("(a p) c -> p a c", p=128),
                                  zTt[:, 0:n // 128, :], r=['zTt'], w=[f'ZT{cc}_{t0}'])
                        return f
                    pend[0] = mk_tr()
            if pend[0] is not None:
                pend[0]()
            S.barrier()
        for (L, nm, nseq, tbase) in ((LS, 's', 1, 0), (LP, 'p', 2, 2048)):
            NT = L // 128
            with ExitStack() as st:
                posT = sb(st, 'posT', [33, L])
                fw1 = sb(st, 'fw1', [33, 64])
                fw2 = sb(st, 'fw2', [64, 64])
                fvec = sb(st, 'fvec', [64, 6])
                hid1 = sb(st, 'hid1', [64, L])
                hid2 = sb(st, 'hid2', [64, L], BF16)
                sint = sb(st, 'sint', [64, 1024])
                S.dma('sp', posT[:], I['posT' + nm][:, :], w=['posT'])
                S.dma('sp', fw1[:], I['hy_f_w1'][j, :, :], w=['fw'])
                S.dma('sp', fw2[:], I['hy_f_w2'][j, :, :], w=['fw'])
                S.dma('sp', fvec[:, 0:3], I['hy_fvecT'][j, :, :], w=['fvec'])
                S.tt('dve', fvec[:, 3:4], fvec[:, 0:1], fvec[:, 1:2], ALU.mult, r=['fvec'], w=['fvec'])
                S.tt('dve', fvec[:, 4:5], fvec[:, 2:3], fvec[:, 1:2], ALU.mult, r=['fvec'], w=['fvec'])
                for c0 in range(0, L, 512):
                    n = min(512, L - c0)
                    S.mm(ps[0][0:64, :n], fw1[:, :], posT[:, c0:c0 + n], True, True, r=['fw', 'posT'], w=['ps0'])
                    sin_rr(hid1[:, c0:c0 + n], ps[0][0:64, :n], n, fvec[:, 1:2], fvec[:, 3:4], sint, ['ps0', 'fvec'], 'hid1')
                    S.mm(ps[1][0:64, :n], fw2[:, :], hid1[:, c0:c0 + n], True, True, r=['fw', 'hid1'], w=['ps1'])
                    sin_rr(hid2[:, c0:c0 + n], ps[1][0:64, :n], n, fvec[:, 1:2], fvec[:, 4:5], sint, ['ps1', 'fvec'], 'hid2')
                fw3 = sb(st, 'fw3', [64, 2 * D], BF16)
                S.dma('pool', fw3[:], I['hy_f_w3'][j, :, :], w=['fw3'])
                hs = sb(st, 'hs', [128, NT, 256], BF16)
                hd_ = sb(st, 'hd', [128, NT, 256], BF16)
                dcts = [sb(st, f'dct{b_}', [128, 256]) for b_ in range(2)]
                absd = sb(st, 'absd', [128, D])
                tlt = sb(st, 'tlt', [128, NT])
                S.dma('sp', absd[:], I['absd'][:, :], w=['absd'])
                S.dma('sp', tlt[:], I['tl' + nm][:, :], w=['absd'])
                hfs = [sb(st, f'hf{b_}', [128, 2, 256]) for b_ in range(2)]
                zT = sb(st, 'zT', [128, NT, nseq * 256], BF16)
                FTs = [sb(st, f'FT{b_}', [128, 2, NT, 128], BF16) for b_ in range(2)]
                AB = sb(st, 'AB', [128, NT, 2, nseq * 256], BF16)
                Kfs = [sb(st, f'Kf{b_}', [128, 2, 256]) for b_ in range(2)]
                m_ = [sb(st, f'm{b}', [128, 256]) for b in range(4)]
                ITs = [sb(st, f'IT{b_}', [128, 2, NT, 256], BF16) for b_ in range(2)]
                yvs = [sb(st, f'yv{b_}', [128, 256]) for b_ in range(2)]
                x0ts = [sb(st, f'x0t{b_}', [128, 256]) for b_ in range(2)]
                zzs = [sb(st, f'zz{b_}', [128, 256]) for b_ in range(2)]
                gts = [sb(st, f'gt{b_}', [128, 256], BF16) for b_ in range(2)]
                inv_it = [0]

                def ldxz(rd_, tc_, sq__, dc_, b_, L=L, tbase=tbase):
                    ch_ = rd_ * 2 + dc_
                    tg_ = tbase + sq__ * L + tc_ * 256
                    S.dma('sp', x0ts[b_][:], X0d[ch_, :, tg_:tg_ + 256], r=['X0all'], w=[f'x0t{b_}'])
                    S.dma('sp', zzs[b_][:], Zd[ch_, :, tg_:tg_ + 256], r=['Zall'], w=[f'zz{b_}'])

                def inv_next(rd_, tc_, sq__, dc_, L=L, nseq=nseq):
                    if dc_ == 0:
                        return (rd_, tc_, sq__, 1)
                    if sq__ + 1 < nseq:
                        return (rd_, tc_, sq__ + 1, 0)
                    if tc_ + 1 < L // 256:
                        return (rd_, tc_ + 1, 0, 0)
                    return None
                dbt = sb(st, 'dbt', [128, 8])
                S.dma('sp', dbt[:], I['hy_d_biasT'][:, j, :], w=['dbt'])
                for rd in range(4):
                    c0 = rd * 256
                    for tt in range(NT):
                        dct, hf = dcts[tt % 2], hfs[tt % 2]
                        dk_, hk_ = f'dct{tt % 2}', f'hf{tt % 2}'
                        pa, pb = 2 * (tt % 2), 2 * (tt % 2) + 1
                        S.mm(ps[pa][:, 0:256], hid2[:, tt * 128:(tt + 1) * 128], fw3[:, c0:c0 + 256], True, True,
                             r=['hid2', 'fw3'], w=[PSK[pa]])
                        S.mm(ps[pb][:, 0:256], hid2[:, tt * 128:(tt + 1) * 128], fw3[:, D + c0:D + c0 + 256], True, True,
                             r=['hid2', 'fw3'], w=[PSK[pb]])
                        S.act(dct[:, :], absd[:, c0:c0 + 256], AF.Exp, scale=tlt[:, tt:tt + 1], r=['absd'], w=[dk_])
                        S.tt('dve', hf[:, 0, :], ps[pa][:, 0:256], dct[:, :], ALU.mult, r=[PSK[pa], dk_],
                             w=[hk_ + 'a'])
                        S.tt('dve', hf[:, 1, :], ps[pb][:, 0:256], dct[:, :], ALU.mult, r=[PSK[pb], dk_],
                             w=[hk_ + 'b'])
                        if tt == 0:
                            S.memset('dve', hf[0:1, 1, :], 0.0, r=[hk_ + 'b'], w=[hk_ + 'b'])
                        S.tt('dve', hs[:, tt, :], hf[:, 0, :], hf[:, 1, :], ALU.add, r=[hk_ + 'a', hk_ + 'b'], w=['hs'])
                        S.tt('pool', hd_[:, tt, :], hf[:, 0, :], hf[:, 1, :], ALU.subtract, r=[hk_ + 'a', hk_ + 'b'],
                             w=['hd'])
                    for sq_ in range(nseq):
                        tb = tbase + sq_ * L
                        S.dma('sp', zT[:, :, sq_ * 256:(sq_ + 1) * 256],
                              ZTd[tb:tb + L, c0:c0 + 256].rearrange("(a p) c -> p a c", p=128),
                              r=['ZTall'], w=['zT'])
                    for fc in range(NT):
                        FT = FTs[fc % 2]
                        FTK = f'FT{fc % 2}'
                        S.dma('sp', FT[:, :, :, :], I['FT' + nm][fc].rearrange("p (c a f) -> p c a f", c=2, a=NT),
                              w=[FTK])
                        nz = nseq * 256
                        pb_ = 4 * (fc % 2)
                        Kf = Kfs[fc % 2]
                        KfK = f'Kf{fc % 2}'
                        for cs in range(2):
                            for tt in range(NT):
                                S.mm(ps[pb_ + cs][:, :nz], FT[:, cs, tt, :], zT[:, tt, :], tt == 0, tt == NT - 1,
                                     r=[FTK, 'zT'], w=[PSK[pb_ + cs]], signal=(tt == NT - 1))
                            src = hs if cs == 0 else hd_
                            for tt in range(NT):
                                S.mm(ps[pb_ + 2 + cs][:, :256], FT[:, cs, tt, :], src[:, tt, :], tt == 0, tt == NT - 1,
                                     r=[FTK, 'hs', 'hd'], w=[PSK[pb_ + 2 + cs]], signal=(tt == NT - 1))
                            S.copy('act', Kf[:, cs, :], ps[pb_ + 2 + cs][:, :256], r=[PSK[pb_ + 2 + cs]], w=[KfK])
                        for sq_ in range(nseq):
                            zs = slice(sq_ * 256, (sq_ + 1) * 256)
                            S.tt('dve', m_[0][:], Kf[:, 0, :], ps[pb_][:, zs], ALU.mult, r=[KfK, PSK[pb_]], w=['m0'])
                            S.tt('dve', m_[1][:], Kf[:, 1, :], ps[pb_ + 1][:, zs], ALU.mult, r=[KfK, PSK[pb_ + 1]],
                                 w=['m1'])
                            S.tt('dve', m_[2][:], Kf[:, 0, :], ps[pb_ + 1][:, zs], ALU.mult, r=[KfK, PSK[pb_ + 1]],
                                 w=['m2'])
                            S.tt('dve', m_[3][:], Kf[:, 1, :], ps[pb_][:, zs], ALU.mult, r=[KfK, PSK[pb_]], w=['m3'])
                            S.tt('pool', AB[:, fc, 0, zs], m_[0][:], m_[1][:], ALU.subtract, r=['m0', 'm1'], w=['AB'])
                            S.tt('pool', AB[:, fc, 1, zs], m_[2][:], m_[3][:], ALU.add, r=['m2', 'm3'], w=['AB'])
                    inv_it[0] = 0
                    def ldit(tc_):
                        S.dma('sp', ITs[tc_ % 2][:, :, :, :],
                              I['IT' + nm][tc_].rearrange("p (c a t) -> p c a t", c=2, a=NT), w=[f'IT{tc_ % 2}'])
                    ldit(0)
                    for tc in range(L // 256):
                        IT = ITs[tc % 2]
                        ITK = f'IT{tc % 2}'
                        if tc + 1 < L // 256:
                            ldit(tc + 1)
                        for sq_ in range(nseq):
                            for dc in range(2):
                                ch = rd * 2 + dc
                                pp = ps[4 + dc]
                                cnt = 0
                                for fc in range(NT):
                                    for cs in range(2):
                                        S.mm(pp[:, :256], AB[:, fc, cs, sq_ * 256 + dc * 128:sq_ * 256 + (dc + 1) * 128],
                                             IT[:, cs, fc, :], cnt == 0, cnt == 2 * NT - 1, r=['AB', ITK],
                                             w=[PSK[4 + dc]], signal=(cnt == 2 * NT - 1))
                                        cnt += 1
                                tg = tbase + sq_ * L + tc * 256
                                b_ = inv_it[0] % 2
                                x0t, zz, yv, gt = x0ts[b_], zzs[b_], yvs[b_], gts[b_]
                                if inv_it[0] == 0:
                                    ldxz(rd, tc, sq_, dc, 0)
                                nxt = inv_next(rd, tc, sq_, dc)
                                if nxt is not None:
                                    ldxz(*nxt, (inv_it[0] + 1) % 2)
                                S.ts('dve', zz[:], zz[:], dbt[:, ch:ch + 1], None, ALU.mult, r=[f'zz{b_}', 'dbt'],
                                     w=[f'zz{b_}'])
                                S.stt('dve', yv[:], pp[:, :256], 1.0 / L, zz[:], ALU.mult, ALU.add,
                                      r=[PSK[4 + dc], f'zz{b_}'], w=[f'yv{b_}'])
                                S.tt('pool', gt[:], yv[:], x0t[:], ALU.mult, r=[f'yv{b_}', f'x0t{b_}'], w=[f'gt{b_}'])
                                S.dma('sp', Gd[ch, :, tg:tg + 256], gt[:], r=[f'gt{b_}'], w=[f'G{ch}_{tg}'])
                                inv_it[0] += 1
                S.barrier()
        with ExitStack() as st:
            bo = sb(st, 'bo', [128, 8])
            S.dma('sp', bo[:], I['hy_b_outT'][:, j, :], w=['bvec'])
            proj_out(i, 0, Gd, KC, I['hy_w_out'][j], bo)

    stage_in()
    for (kind, i) in prog:
        if kind == 'mix':
            if i % 2 == 0:
                hyena(i)
            else:
                attn(i)
        else:
            ffn(i)
    stage_out()
    S.barrier()

    with nc.Block() as block:
        def mk(name):
            def body(e):
                for it in S.q[name]:
                    if it[0] == 'w':
                        e.wait_ge(S.sems[it[1]], it[2])
                    else:
                        ins = it[1](e)
                        if it[2] is not None:
                            ins.then_inc(S.sems[it[2]], it[3])
            return body
        block.tensor(mk('pe'))
        block.scalar(mk('act'))
        block.vector(mk('dve'))
        block.gpsimd(mk('pool'))
        block.sync(mk('sp'))
    es.close()
    k.nops = S.nops
    return nc


def _T(v, nch):
    return np.ascontiguousarray(np.asarray(v, np.float32).reshape(nch, 128).T)


def _tables():
    tb = {}
    tb['ident'] = np.eye(128, dtype=np.float32)
    mk_ = np.zeros((128, 2), np.float32)
    mk_[:64, 0] = 1.0
    mk_[64:, 1] = 1.0
    tb['msk'] = mk_
    R = np.zeros((128, 128), np.float32)
    for s in range(2):
        for a in range(2):
            for f in range(16):
                p0 = s * 64 + a * 32 + f
                p1 = p0 + 16
                R[p1, p0] = -1.0
                R[p0, p1] = 1.0
    tb['rotm'] = R
    rows = LS // 64
    r = np.repeat(np.arange(rows, dtype=np.float32), 64)
    cidx = np.tile(np.arange(64, dtype=np.float32), rows)
    inv = (10000.0 ** (-np.arange(16, dtype=np.float32) / 16)).astype(np.float32)
    ar = r[:, None] * inv
    ac = cidx[:, None] * inv
    cos = np.concatenate([np.cos(ar), np.cos(ar), np.cos(ac), np.cos(ac)], -1).astype(np.float32)
    sin = np.concatenate([np.sin(ar), np.sin(ar), np.sin(ac), np.sin(ac)], -1).astype(np.float32)
    tb['ropec'] = np.ascontiguousarray(np.concatenate([cos, cos], -1).T)
    tb['ropes'] = np.ascontiguousarray(np.concatenate([sin, sin], -1).T)
    for L, nm in ((LS, 's'), (LP, 'p')):
        f = np.arange(L, dtype=np.float64)[:, None]
        n = np.arange(L, dtype=np.float64)[None, :]
        ang = np.pi * (2 * f + 1) * n / (2 * L)
        Cq, Sq = np.cos(ang), np.sin(ang)
        NT = L // 128
        CS = np.stack([Cq, Sq], 0).astype(np.float32)
        ft = CS.reshape(2, NT, 128, NT, 128).transpose(1, 4, 0, 3, 2)
        tb['FT' + nm] = np.ascontiguousarray(ft).reshape(NT, 128, 2 * L).astype(ml_dtypes.bfloat16)
        it = CS.reshape(2, NT, 128, L // 256, 256).transpose(3, 2, 0, 1, 4)
        tb['IT' + nm] = np.ascontiguousarray(it).reshape(L // 256, 128, 2 * NT * 256).astype(ml_dtypes.bfloat16)
        t = np.linspace(0.0, 1.0, L, dtype=np.float32)[:, None]
        w = (2.0 * np.float32(math.pi) * np.arange(L, dtype=np.float32)[:, None] / np.float32(L)).astype(np.float32)
        bands = np.linspace(1e-4, 15, 16, dtype=np.float32)
        z = np.concatenate([t, np.cos(bands * w), -np.sin(bands * w)], -1).astype(np.float32)
        tb['posT' + nm] = np.ascontiguousarray(z.T)
        mind = math.log(1e-2) / 1.5
        maxd = math.log(1e-2) / 0.3
        deltas = np.linspace(mind, maxd, D, dtype=np.float32)
        tb['tl' + nm] = np.ascontiguousarray((-t[:, 0]).reshape(L // 128, 128).T).astype(np.float32)
        tb['absd'] = np.ascontiguousarray(np.broadcast_to(np.abs(deltas)[None, :], (128, D))).astype(np.float32)
    return tb


_CACHE = {}


def _run(inputs, prog, dbg_x=None):
    key = tuple(prog)
    if key not in _CACHE:
        _CACHE[key] = build(prog)
    nc = _CACHE[key]
    tb = _tables()
    g = {k: np.asarray(v) for k, v in inputs.items()}
    shared = {
        'w_ada': g['w_ada'], 'b_adaT': np.ascontiguousarray(g['b_ada'].reshape(4, 48, 128).transpose(2, 0, 1)),
        'norm_wT': np.ascontiguousarray(g['norm_w'].reshape(4, 4, 8, 128).transpose(3, 0, 1, 2)),
        'hy_w_in': g['hy_w_in'],
        'hy_b_inT': np.ascontiguousarray(g['hy_b_in'].reshape(2, 24, 128).transpose(2, 0, 1)),
        'hy_w_shortT': np.ascontiguousarray(g['hy_w_short'].reshape(2, 3, 24, 128).transpose(3, 0, 1, 2)),
        'hy_b_shortT': np.ascontiguousarray(g['hy_b_short'].reshape(2, 24, 128).transpose(2, 0, 1)),
        'hy_f_w1': g['hy_f_w1'], 'hy_f_w2': g['hy_f_w2'], 'hy_f_w3': g['hy_f_w3'],
        'hy_fvecT': np.ascontiguousarray(np.stack([g['hy_f_b1'], g['hy_f_freq'], g['hy_f_b2']], -1)).astype(np.float32),
        'hy_d_biasT': np.ascontiguousarray(g['hy_d_bias'].reshape(2, 8, 128).transpose(2, 0, 1)),
        'hy_w_out': g['hy_w_out'],
        'hy_b_outT': np.ascontiguousarray(g['hy_b_out'].reshape(2, 8, 128).transpose(2, 0, 1)),
        'at_w_qkv': g['at_w_qkv'], 'at_w_out': g['at_w_out'],
        'lam_bc': np.ascontiguousarray(np.broadcast_to(
            np.stack([g['at_lambda_q1'], g['at_lambda_k1'], g['at_lambda_q2'], g['at_lambda_k2']], 1)[None],
            (128, 2, 4, 64))).astype(np.float32),
        'subln_bc': np.ascontiguousarray(np.broadcast_to(g['at_subln'][None], (128, 2, 128))).astype(np.float32),
        'ffn_w_up': g['ffn_w_up'],
        'ffn_w_dwT': np.ascontiguousarray(g['ffn_w_dw'].reshape(4, 3, 44, 128).transpose(3, 0, 1, 2)),
        'ffn_b_dwT': np.ascontiguousarray(g['ffn_b_dw'].reshape(4, 44, 128).transpose(2, 0, 1)),
        'ffn_w_down': g['ffn_w_down'],
    }
    shared.update(tb)
    in_maps = []
    for c in range(8):
        b = c // 4
        m = dict(shared)
        if dbg_x is not None:
            m['xin'] = dbg_x
        else:
            m['xin'] = np.ascontiguousarray(np.concatenate(
                [g['x_sample'][b], g['x_prompt'][2 * c], g['x_prompt'][2 * c + 1]], 0)).astype(np.float32)
        m['condT'] = np.ascontiguousarray(np.stack([_T(g['c'][b], 8), _T(g['c_ctx'], 8)], -1))
        m['cache_k'] = np.ascontiguousarray(g['cache_k'][b])
        m['cache_v'] = np.ascontiguousarray(g['cache_v'][b])
        in_maps.append(m)
    import os
    ncores = int(os.environ.get('K_NCORES', '8'))
    if os.environ.get('K_TRACE'):
        res = run_bass_kernel_spmd(nc, in_maps[:ncores], core_ids=list(range(ncores)), trace=True)
        print("EXEC_TIME_NS", res.exec_time_ns)
    else:
        res = run_bass_kernel_spmd(nc, in_maps[:ncores], core_ids=list(range(ncores)))
    return list(res.results) + [res.results[0]] * (8 - ncores)


FULL = [('mix', 0), ('ffn', 0), ('mix', 1), ('ffn', 1), ('mix', 2), ('ffn', 2), ('mix', 3), ('ffn', 3)]


def kernel(**inputs):
    r = _run(inputs, FULL)
    yp = np.zeros((16, 256, D), np.float32)
    ys = np.zeros((2, 2048, D), np.float32)
    nk = np.zeros((16, 2, 8, 256, 128), np.float32)
    nv = np.zeros((16, 2, 8, 256, 128), np.float32)
    for c in range(8):
        yo = np.asarray(r[c]['yout'])
        yp[2 * c] = yo[2048:2304]
        yp[2 * c + 1] = yo[2304:2560]
        if c % 4 == 0:
            ys[c // 4] = yo[:2048]
        nk[2 * c:2 * c + 2] = np.asarray(r[c]['nk'])
        nv[2 * c:2 * c + 2] = np.asarray(r[c]['nv'])
    return (yp, ys, nk, nv)
```

```python
import math
import os
from contextlib import ExitStack
import numpy as np
import ml_dtypes
import concourse.bass as bass
import concourse.mybir as mybir
from concourse.bass_utils import run_bass_kernel_spmd

F32, BF16 = mybir.dt.float32, mybir.dt.bfloat16
AF = mybir.ActivationFunctionType
ALU = mybir.AluOpType
D = 1024
KC = 8
DFF = 2816
T = 2560
LS, LP = 2048, 256
EPS = 1e-6
CH = [(0, 512, 1, 0, 0), (512, 512, 0, 0, 0), (1024, 512, 0, 0, 0), (1536, 512, 0, 1, 0),
      (2048, 256, 1, 1, 1), (2304, 256, 1, 1, 1)]
BLKENG = {'pe': 'tensor', 'act': 'scalar', 'dve': 'vector', 'pool': 'gpsimd', 'sp': 'sync'}


class Sched:
    def __init__(self, nc, es):
        self.nc, self.es = nc, es
        self.E = ['pe', 'act', 'dve', 'pool', 'sp']
        self.q = {e: [] for e in self.E}
        self.sems = []
        self.cur = {}
        self.cnt = {}
        for e in ['pe', 'act', 'dve', 'pool']:
            self.cur[e] = self.newsem()
            self.cnt[e] = 0
        self.waited = {e: {} for e in self.E}
        self.lastw, self.readers = {}, {}
        self.dsem = {e: [self.newsem() for _ in range(8)] for e in ['sp', 'pool']}
        self.dval = {e: [0] * 8 for e in ['sp', 'pool']}
        self.dcnt = {'sp': 0, 'pool': 0}
        self.live = set()
        self.nops = 0

    def newsem(self):
        self.sems.append(self.es.enter_context(self.nc.semaphore(f"s{len(self.sems)}")))
        return len(self.sems) - 1

    def _wait(self, eng, tok):
        si, v, en = tok
        if en == eng and (eng == 'pe' or os.environ.get('K_NOSAME')):
            return
        if self.waited[eng].get(si, 0) >= v:
            return
        self.q[eng].append(('w', si, v))
        self.waited[eng][si] = v

    def op(self, eng, fn, r=(), w=(), dma=False, signal=True):
        self.nops += 1
        psr = [k_ for k_ in r if k_[:2] == 'ps' and k_[2:].isdigit()]
        if psr:
            r = [k_ for k_ in r if k_ not in psr]
            w = list(w) + psr
        deps = []
        for k in r:
            t = self.lastw.get(k)
            if t:
                deps.append(t)
        for k in w:
            t = self.lastw.get(k)
            if t:
                deps.append(t)
            deps.extend(self.readers.get(k, ()))
        for t in deps:
            self._wait(eng, t)
        if dma:
            i = self.dcnt[eng]
            slot = i % 8
            si = self.dsem[eng][slot]
            prev = self.dval[eng][slot]
            if prev > 0:
                self._wait(eng, (si, prev, 'dma'))
            self.dval[eng][slot] = prev + 16
            self.dcnt[eng] += 1
            tok = (si, prev + 16, 'dma')
            self.q[eng].append(('o', fn, si, 16))
        elif signal:
            if self.cnt[eng] >= 30000:
                self.cur[eng] = self.newsem()
                self.cnt[eng] = 0
            self.cnt[eng] += 1
            tok = (self.cur[eng], self.cnt[eng], eng)
            self.q[eng].append(('o', fn, self.cur[eng], 1))
        else:
            self.q[eng].append(('o', fn, None, 0))
            return None
        self.live.add(tok)
        for k in w:
            self.lastw[k] = tok
            self.readers[k] = []
        for k in r:
            self.readers.setdefault(k, []).append(tok)
        return tok

    def barrier(self):
        latest = {}
        for t in self.live:
            if latest.get(t[0], (0, 0, 0))[1] < t[1]:
                latest[t[0]] = t
        for e in self.E:
            for t in latest.values():
                if not (t[2] == e and e == 'pe'):
                    si, v, en = t
                    if self.waited[e].get(si, 0) < v:
                        self.q[e].append(('w', si, v))
                        self.waited[e][si] = v
        self.live = set(latest.values())
        self.lastw, self.readers = {}, {}

    def dma(self, q, out, in_, r=(), w=()):
        return self.op(q, lambda e: e.dma_start(out=out, in_=in_), r, w, dma=True)

    def mm(self, out, lhsT, rhs, start, stop, r=(), w=(), signal=True):
        return self.op('pe', lambda e: e.matmul(out, lhsT, rhs, start=start, stop=stop), r, w, signal=signal)

    def tr(self, out, in_, ident, r=(), w=()):
        return self.op('pe', lambda e: e.transpose(out, in_, ident), r, w)

    def act(self, out, in_, func, bias=0.0, scale=1.0, r=(), w=(), accum=None):
        if accum is None:
            return self.op('act', lambda e: e.activation(out=out, in_=in_, func=func, bias=bias, scale=scale), r, w)
        return self.op('act', lambda e: e.activation(out=out, in_=in_, func=func, bias=bias, scale=scale,
                                                     accum_out=accum), r, w)

    def ts(self, eng, out, in0, s1, s2, op0, op1=None, r=(), w=()):
        if op1 is None:
            return self.op(eng, lambda e: e.tensor_scalar(out=out, in0=in0, scalar1=s1, scalar2=None, op0=op0), r, w)
        return self.op(eng, lambda e: e.tensor_scalar(out=out, in0=in0, scalar1=s1, scalar2=s2, op0=op0, op1=op1), r, w)

    def tt(self, eng, out, in0, in1, op, r=(), w=()):
        return self.op(eng, lambda e: e.tensor_tensor(out=out, in0=in0, in1=in1, op=op), r, w)

    def stt(self, eng, out, in0, scalar, in1, op0, op1, r=(), w=()):
        return self.op(eng, lambda e: e.scalar_tensor_tensor(out=out, in0=in0, scalar=scalar, in1=in1,
                                                             op0=op0, op1=op1), r, w)

    def copy(self, eng, out, in_, r=(), w=()):
        if eng == 'act':
            return self.act(out, in_, AF.Identity, r=r, w=w)
        return self.op(eng, lambda e: e.tensor_copy(out=out, in_=in_), r, w)

    def recip(self, out, in_, r=(), w=()):
        return self.op('dve', lambda e: e.reciprocal(out=out, in_=in_), r, w)

    def memset(self, eng, ap, val, r=(), w=()):
        return self.op(eng, lambda e: e.memset(ap, val), r, w)


class K:
    pass


def build(prog, dbg=False):
    nc = bass.Bass("TRN2", target_bir_lowering=False)
    es = ExitStack()
    k = K()
    k.nc = nc

    def din(name, shape, dt=F32):
        return nc.dram_tensor(name, list(shape), dt, kind="ExternalInput").ap()

    def dout(name, shape, dt=F32):
        return nc.dram_tensor(name, list(shape), dt, kind="ExternalOutput").ap()

    def dscr(name, shape, dt=F32):
        return nc.dram_tensor(name, list(shape), dt, kind="Internal").ap()

    I = {}
    I['xin'] = din('xin', [T, D])
    I['condT'] = din('condT', [128, KC, 2])
    I['w_ada'] = din('w_ada', [4, D, 6 * D])
    I['b_adaT'] = din('b_adaT', [128, 4, 48])
    I['norm_wT'] = din('norm_wT', [128, 4, 4, 8])
    I['hy_w_in'] = din('hy_w_in', [2, D, 3 * D])
    I['hy_b_inT'] = din('hy_b_inT', [128, 2, 24])
    I['hy_w_shortT'] = din('hy_w_shortT', [128, 2, 3, 24])
    I['hy_b_shortT'] = din('hy_b_shortT', [128, 2, 24])
    I['hy_f_w1'] = din('hy_f_w1', [2, 33, 64])
    I['hy_fvecT'] = din('hy_fvecT', [2, 64, 3])
    I['hy_f_w2'] = din('hy_f_w2', [2, 64, 64])
    I['hy_f_w3'] = din('hy_f_w3', [2, 64, 2 * D])
    I['hy_d_biasT'] = din('hy_d_biasT', [128, 2, 8])
    I['hy_w_out'] = din('hy_w_out', [2, D, D])
    I['hy_b_outT'] = din('hy_b_outT', [128, 2, 8])
    I['at_w_qkv'] = din('at_w_qkv', [2, D, 3 * D])
    I['at_w_out'] = din('at_w_out', [2, D, D])
    I['lam_bc'] = din('lam_bc', [128, 2, 4, 64])
    I['subln_bc'] = din('subln_bc', [128, 2, 128])
    I['ffn_w_up'] = din('ffn_w_up', [4, D, 2 * DFF])
    I['ffn_w_dwT'] = din('ffn_w_dwT', [128, 4, 3, 44])
    I['ffn_b_dwT'] = din('ffn_b_dwT', [128, 4, 44])
    I['ffn_w_down'] = din('ffn_w_down', [4, DFF, D])
    I['cache_k'] = din('cache_k', [2, 8, 256, 128])
    I['cache_v'] = din('cache_v', [2, 8, 256, 128])
    I['ident'] = din('ident', [128, 128])
    I['msk'] = din('msk', [128, 2])
    I['rotm'] = din('rotm', [128, 128])
    I['ropec'] = din('ropec', [128, LS])
    I['ropes'] = din('ropes', [128, LS])
    for L, nm in ((LS, 's'), (LP, 'p')):
        I['FT' + nm] = din('FT' + nm, [L // 128, 128, 2 * L], BF16)
        I['IT' + nm] = din('IT' + nm, [L // 256, 128, 2 * (L // 128) * 256], BF16)
        I['posT' + nm] = din('posT' + nm, [33, L])
        I['tl' + nm] = din('tl' + nm, [128, L // 128])
    I['absd'] = din('absd', [128, D])
    O = {}
    O['yout'] = dout('yout', [T, D])
    O['nk'] = dout('nk', [2, 2, 8, 256, 128])
    O['nv'] = dout('nv', [2, 2, 8, 256, 128])
    X = dscr('X', [KC, 128, T])
    Gd = dscr('Gd', [KC, 128, T], BF16)
    Ad = dscr('Ad', [22, 128, T], BF16)
    X0d = dscr('X0d', [KC, 128, T])
    Zd = dscr('Zd', [KC, 128, T])
    ZTd = dscr('ZTd', [T, D], BF16)
    Yd = dscr('Yd', [KC, 128, T])

    S = Sched(nc, es)

    uid = [0]

    def sb(st, name, shape, dt=F32):
        uid[0] += 1
        return st.enter_context(nc.sbuf_tensor(f"t{uid[0]}_{name}", list(shape), dt))

    ps = [es.enter_context(nc.psum_tensor(f"ps{i}", [128, 512], F32)) for i in range(8)]
    PSK = [f"ps{i}" for i in range(8)]

    ident = sb(es, 'ident', [128, 128])
    ones_bf = sb(es, 'ones_bf', [128, 128], BF16)
    DER = sb(es, 'DER', [128, 4, 6, 8, 2])
    normw = sb(es, 'normw', [128, 4, 4, 8])
    xb = [sb(es, f'xb{i}', [128, KC, 512]) for i in range(2)]
    sq = sb(es, 'sq', [128, KC, 512], BF16)
    rstd = sb(es, 'rstd', [128, 512])
    tmp = [sb(es, f'tmp{i}', [128, 512]) for i in range(2)]
    ybuf = sb(es, 'ybuf', [128, KC, 512])

    S.dma('sp', ident[:], I['ident'][:, :], w=['ident'])
    S.memset('dve', ones_bf[:], 1.0, w=['ones'])
    S.dma('sp', normw[:], I['norm_wT'][:, :, :, :], w=['normw'])

    with ExitStack() as st:
        MOD = sb(st, 'MOD', [128, 4, 48, 2])
        condT = sb(st, 'condT', [128, KC, 2])
        scond = sb(st, 'scond', [128, KC, 2], BF16)
        badaT = sb(st, 'badaT', [128, 4, 48])
        wada = [sb(st, f'wada{i}', [128, KC, 512], BF16) for i in range(3)]
        S.dma('sp', condT[:], I['condT'][:, :, :], w=['condT'])
        S.dma('sp', badaT[:], I['b_adaT'][:, :, :], w=['badaT'])
        S.act(scond[:], condT[:], AF.Silu, r=['condT'], w=['scond'])
        blk = 0
        for i in range(4):
            for nb in range(12):
                wt = wada[blk % 3]
                wk = f'wada{blk % 3}'
                S.dma('pool', wt[:, :, :],
                      I['w_ada'][i, :, nb * 512:(nb + 1) * 512].rearrange("(k p) n -> p k n", p=128), w=[wk])
                for m in range(4):
                    fc = nb * 4 + m
                    pt = ps[fc % 2]
                    for kc in range(KC):
                        S.mm(pt[:, 0:2], wt[:, kc, m * 128:(m + 1) * 128], scond[:, kc, :], kc == 0, kc == KC - 1,
                             r=[wk, 'scond'], w=[PSK[fc % 2]], signal=(kc == KC - 1))
                    S.ts('dve', MOD[:, i, fc, :], pt[:, 0:2], badaT[:, i, fc:fc + 1], None, ALU.add,
                         r=[PSK[fc % 2], 'badaT'], w=['MOD'])
                blk += 1
        for i in range(4):
            for half in range(2):
                b = half * 24
                for kc in range(KC):
                    S.ts('dve', DER[:, i, half * 3 + 0, kc, :], MOD[:, i, b + 8 + kc, :], 1.0,
                         normw[:, i, half * 2, kc:kc + 1], ALU.add, ALU.mult, r=['MOD', 'normw'], w=['DER'])
                    S.copy('dve', DER[:, i, half * 3 + 1, kc, :], MOD[:, i, b + kc, :], r=['MOD'], w=['DER'])
                    S.ts('dve', DER[:, i, half * 3 + 2, kc, :], MOD[:, i, b + 16 + kc, :],
                         normw[:, i, half * 2 + 1, kc:kc + 1], None, ALU.mult, r=['MOD', 'normw'], w=['DER'])
        S.barrier()

    def xview(t0, n):
        return X[:, :, t0:t0 + n].rearrange("k p t -> p k t")

    def rms_rstd(src, n, rk, feat=1024.0):
        S.act(sq[:, :, :n], src, AF.Square, r=rk, w=['sq'])
        for kc in range(KC):
            S.mm(ps[7][:, :n], ones_bf[:], sq[:, kc, :n], kc == 0, kc == KC - 1, r=['sq', 'ones'], w=['ps7'],
                 signal=(kc == KC - 1))
        S.ts('dve', rstd[:, :n], ps[7][:, :n], 1.0 / feat, EPS, ALU.mult, ALU.add, r=['ps7'], w=['rstd'])
        S.act(rstd[:, :n], rstd[:, :n], AF.Sqrt, r=['rstd'], w=['rstd'])
        S.recip(rstd[:, :n], rstd[:, :n], r=['rstd'], w=['rstd'])

    def stage_in():
        for ci in range(5):
            t0 = ci * 512
            xt = xb[ci % 2]
            xk = f'xb{ci % 2}'
            with ExitStack() as st:
                pass
            for tt in range(4):
                S.dma('sp', ybuf[:, 2 * tt:2 * tt + 2, :].rearrange("p a b -> p (a b)"),
                      I['xin'][t0 + tt * 128:t0 + (tt + 1) * 128, :], w=[f'yb{tt}'])
            for tt in range(4):
                for kc in range(KC):
                    pi = (tt * KC + kc) % 4
                    src = ybuf[:, 2 * tt:2 * tt + 2, :].rearrange("p a b -> p (a b)")[:, kc * 128:(kc + 1) * 128]
                    S.tr(ps[pi][:, 0:128], src, ident[:], r=[f'yb{tt}', 'ident'], w=[PSK[pi]])
                    S.copy('dve' if kc % 2 else 'act', xt[:, kc, tt * 128:(tt + 1) * 128], ps[pi][:, 0:128],
                           r=[PSK[pi]], w=[xk])
            S.dma('sp', xview(t0, 512), xt[:, :, :], r=[xk], w=[f'X{t0}'])
        S.barrier()

    def stage_out():
        for ci in range(5):
            t0 = ci * 512
            xt = xb[ci % 2]
            xk = f'xb{ci % 2}'
            S.dma('sp', xt[:, :, :], xview(t0, 512), r=[f'X{t0}'], w=[xk])
            for tt in range(4):
                yv = ybuf[:, 2 * tt:2 * tt + 2, :].rearrange("p a b -> p (a b)")
                for kc in range(KC):
                    pi = (tt * KC + kc) % 4
                    S.tr(ps[pi][:, 0:128], xt[:, kc, tt * 128:(tt + 1) * 128], ident[:], r=[xk, 'ident'], w=[PSK[pi]])
                    S.copy('dve' if kc % 2 else 'act', yv[:, kc * 128:(kc + 1) * 128], ps[pi][:, 0:128],
                           r=[PSK[pi]], w=[f'yb{tt}'])
                tk = S.dma('sp', O['yout'][t0 + tt * 128:t0 + (tt + 1) * 128, :], yv, r=[f'yb{tt}'], w=[f'yo{ci}_{tt}'])
        S.barrier()

    def norm_to_hall(st, i, half):
        h_all = sb(st, 'h_all', [128, KC, T], BF16)
        for ci, (t0, n, lz, rz, cond) in enumerate(CH):
            xt = xb[ci % 2]
            xk = f'xb{ci % 2}'
            S.dma('sp', xt[:, :, :n], xview(t0, n), r=[f'X{t0}'], w=[xk])
            rms_rstd(xt[:, :, :n], n, [xk])
            for kc in range(KC):
                tp = tmp[kc % 2]
                S.tt('dve', tp[:, :n], xt[:, kc, :n], rstd[:, :n], ALU.mult, r=[xk, 'rstd'], w=[f'tmp{kc % 2}'])
                S.act(h_all[:, kc, t0:t0 + n], tp[:, :n], AF.Identity,
                      bias=DER[:, i, half * 3 + 1, kc, cond:cond + 1], scale=DER[:, i, half * 3 + 0, kc, cond:cond + 1],
                      r=[f'tmp{kc % 2}', 'DER'], w=[f'h{t0}'])
        return h_all

    def hkeys(t0, n):
        return [f'h{c[0]}' for c in CH if c[0] + c[1] >= t0 - 1 and c[0] <= t0 + n]

    def proj_out(i, half, src, nk, Wd, biasT, wres=None):
        with ExitStack() as st:
            if wres is None:
                wres = sb(st, 'wres', [128, nk, D], BF16)
                S.dma('pool', wres[:, :, :], Wd.rearrange("(k p) n -> p k n", p=128), w=['wres'])
            gcs = [sb(st, f'gc{b_}', [128, nk, 512], BF16) for b_ in range(2)]

            def ld_(c_):
                t0_, n_ = CH[c_][0], CH[c_][1]
                S.dma('sp', gcs[c_ % 2][:, :, :n_], src[:, :, t0_:t0_ + n_].rearrange("k p t -> p k t"),
                      r=[f'G{t0_}'], w=[f'gc{c_ % 2}'])
                S.dma('sp', xb[c_ % 2][:, :, :n_], xview(t0_, n_), r=[f'X{t0_}'], w=[f'xb{c_ % 2}'])
            ld_(0)
            for ci, (t0, n, lz, rz, cond) in enumerate(CH):
                xt = xb[ci % 2]
                xk = f'xb{ci % 2}'
                gc = gcs[ci % 2]
                gck = f'gc{ci % 2}'
                if ci + 1 < len(CH):
                    ld_(ci + 1)
                for oc in range(KC):
                    pt = ps[oc % 4]
                    for kc in range(nk):
                        S.mm(pt[:, :n], wres[:, kc, oc * 128:(oc + 1) * 128], gc[:, kc, :n], kc == 0, kc == nk - 1,
                             r=['wres', gck], w=[PSK[oc % 4]], signal=(kc == nk - 1))
                    bias = 0.0 if biasT is None else biasT[:, oc:oc + 1]
                    S.act(ybuf[:, oc, :n], pt[:, :n], AF.Identity, bias=bias, r=[PSK[oc % 4], 'bvec'], w=['ybuf'])
                rms_rstd(ybuf[:, :, :n], n, ['ybuf'])
                for oc in range(KC):
                    tp = tmp[oc % 2]
                    S.tt('dve', tp[:, :n], ybuf[:, oc, :n], rstd[:, :n], ALU.mult, r=['ybuf', 'rstd'],
                         w=[f'tmp{oc % 2}'])
                    S.stt('dve', xt[:, oc, :n], tp[:, :n], DER[:, i, half * 3 + 2, oc, cond:cond + 1], xt[:, oc, :n],
                          ALU.mult, ALU.add, r=[f'tmp{oc % 2}', 'DER', xk], w=[xk])
                S.dma('sp', xview(t0, n), xt[:, :, :n], r=[xk], w=[f'X{t0}'])
            S.barrier()

    def conv3(psm, psh, n, lz, rz, b_in, w0, w1, w2, b_out, ub, out, pk, hk, ubk, outk):
        bi = 0.0 if b_in is None else b_in
        S.act(ub[:, 1:n + 1], psm, AF.Identity, bias=bi, r=[pk, 'cvec'], w=[ubk])
        if lz:
            S.memset('pool', ub[:, 0:1], 0.0, w=[ubk])
        else:
            S.act(ub[:, 0:1], psh[:, 0:1], AF.Identity, bias=bi, r=[hk, 'cvec'], w=[ubk])
        if rz:
            S.memset('pool', ub[:, n + 1:n + 2], 0.0, w=[ubk])
        else:
            S.act(ub[:, n + 1:n + 2], psh[:, 1:2], AF.Identity, bias=bi, r=[hk, 'cvec'], w=[ubk])
        S.ts('dve', out[:, :n], ub[:, 1:n + 1], w1, b_out, ALU.mult, ALU.add, r=[ubk, 'cvec'], w=[outk])
        S.stt('dve', out[:, :n], ub[:, 0:n], w0, out[:, :n], ALU.mult, ALU.add, r=[ubk, 'cvec', outk], w=[outk])
        S.stt('dve', out[:, :n], ub[:, 2:n + 2], w2, out[:, :n], ALU.mult, ALU.add, r=[ubk, 'cvec', outk], w=[outk])

    def conv3f(psm, psh, n, lz, rz, w0, w1, w2, b_out, out, pk, hk, outk):
        S.act(out[:, :n], psm, AF.Identity, bias=b_out, scale=w1, r=[pk, 'cvec', 'cvec2'], w=[outk])
        S.stt('dve', out[:, 1:n], psm[:, 0:n - 1], w0, out[:, 1:n], ALU.mult, ALU.add, r=[pk, 'cvec', outk], w=[outk])
        S.stt('dve', out[:, 0:n - 1], psm[:, 1:n], w2, out[:, 0:n - 1], ALU.mult, ALU.add, r=[pk, 'cvec', outk],
              w=[outk])
        if not lz:
            S.stt('dve', out[:, 0:1], psh[:, 0:1], w0, out[:, 0:1], ALU.mult, ALU.add, r=[hk, 'cvec', outk], w=[outk])
        if not rz:
            S.stt('dve', out[:, n - 1:n], psh[:, 1:2], w2, out[:, n - 1:n], ALU.mult, ALU.add, r=[hk, 'cvec', outk],
                  w=[outk])

    def lin_chunk(h_all, wt, wk, col0, t0, n, lz, rz, pm, ph):
        hk = hkeys(t0, n)
        for kc in range(KC):
            S.mm(ps[pm][:, :n], wt[:, kc, col0:col0 + 128], h_all[:, kc, t0:t0 + n], kc == 0, kc == KC - 1,
                 r=[wk] + hk, w=[PSK[pm]], signal=(kc == KC - 1))
        if not (lz and rz):
            a = t0 - 1 if not lz else t0
            b = t0 + n if not rz else t0 + n - 1
            for kc in range(KC):
                S.mm(ps[ph][:, 0:2], wt[:, kc, col0:col0 + 128], h_all[:, kc, a:b + 1:b - a], kc == 0, kc == KC - 1,
                     r=[wk] + hk, w=[PSK[ph]], signal=(kc == KC - 1))

    def ffn(i):
      with ExitStack() as st0:
        wres_ = sb(st0, 'wresf', [128, 22, D], BF16)
        with ExitStack() as st:
            h_all = norm_to_hall(st, i, 1)
            wdw = sb(st, 'wdw', [128, 3, 44])
            bdw = sb(st, 'bdw', [128, 44])
            S.dma('sp', wdw[:], I['ffn_w_dwT'][:, i, :, :], w=['cvec'])
            S.dma('sp', bdw[:], I['ffn_b_dwT'][:, i, :], w=['cvec'])
            wg = [sb(st, f'wg{b}', [128, KC, 256], BF16) for b in range(2)]
            wv = [sb(st, f'wv{b}', [128, KC, 256], BF16) for b in range(2)]
            cgs = [sb(st, f'cg{b_}', [128, 512]) for b_ in range(2)]
            cvs = [sb(st, f'cv{b_}', [128, 512]) for b_ in range(2)]
            sgs = [sb(st, f'sg{b_}', [128, 512]) for b_ in range(2)]
            at = [sb(st, f'at{b}', [128, 512], BF16) for b in range(2)]
            Wup = I['ffn_w_up']
            it = 0
            for jb in range(11):
                b = jb % 2
                S.dma('pool', wg[b][:, :, :],
                      Wup[i, :, jb * 256:(jb + 1) * 256].rearrange("(k p) n -> p k n", p=128), w=[f'wg{b}'])
                S.dma('pool', wv[b][:, :, :],
                      Wup[i, :, DFF + jb * 256:DFF + (jb + 1) * 256].rearrange("(k p) n -> p k n", p=128),
                      w=[f'wv{b}'])
                if jb == 2:
                    S.dma('pool', wres_[:, :, :], I['ffn_w_down'][i].rearrange("(k p) n -> p k n", p=128),
                          w=['wres'])
                for jj in range(2):
                    j = jb * 2 + jj
                    for ci, (t0, n, lz, rz, cond) in enumerate(CH):
                        p0 = 4 * (it % 2)
                        lin_chunk(h_all, wg[b], f'wg{b}', jj * 128, t0, n, lz, rz, p0, p0 + 1)
                        lin_chunk(h_all, wv[b], f'wv{b}', jj * 128, t0, n, lz, rz, p0 + 2, p0 + 3)
                        cg, cv, sg = cgs[it % 2], cvs[it % 2], sgs[it % 2]
                        cgk, cvk, sgk = f'cg{it % 2}', f'cv{it % 2}', f'sg{it % 2}'
                        conv3f(ps[p0][:, :n], ps[p0 + 1], n, lz, rz, wdw[:, 0, j:j + 1], wdw[:, 1, j:j + 1],
                               wdw[:, 2, j:j + 1], bdw[:, j:j + 1], cg, PSK[p0], PSK[p0 + 1], cgk)
                        conv3f(ps[p0 + 2][:, :n], ps[p0 + 3], n, lz, rz, wdw[:, 0, 22 + j:23 + j],
                               wdw[:, 1, 22 + j:23 + j], wdw[:, 2, 22 + j:23 + j], bdw[:, 22 + j:23 + j], cv,
                               PSK[p0 + 2], PSK[p0 + 3], cvk)
                        S.act(sg[:, :n], cg[:, :n], AF.Silu, r=[cgk], w=[sgk])
                        a = at[it % 2]
                        S.tt('pool', a[:, :n], sg[:, :n], cv[:, :n], ALU.mult, r=[sgk, cvk], w=[f'at{it % 2}'])
                        S.dma('sp', Ad[j, :, t0:t0 + n], a[:, :n], r=[f'at{it % 2}'], w=[f'A{j}_{t0}'])
                        it += 1
            S.barrier()
        proj_out(i, 1, Ad, 22, I['ffn_w_down'][i], None, wres=wres_)

    def attn(i):
        j = i // 2
        lam_init = 0.8 - 0.6 * math.exp(-0.3 * i)
        with ExitStack() as st:
            h_all = norm_to_hall(st, i, 0)
            lams = sb(st, 'lams', [128, 4])
            subw = sb(st, 'subw', [128, 128])
            rcts = [sb(st, f'rc{b_}', [128, 512]) for b_ in range(2)]
            rsts = [sb(st, f'rs{b_}', [128, 512]) for b_ in range(2)]
            S.dma('sp', subw[:], I['subln_bc'][:, j, :], w=['subw'])
            wrot = [sb(st, f'wrot{b}', [128, KC, 128], BF16) for b in range(2)]
            qT = sb(st, 'qT', [128, T], BF16)
            kT = sb(st, 'kT', [128, 256 + T], BF16)
            kT2 = sb(st, 'kT2', [128, 256 + T], BF16)
            msk = sb(st, 'msk', [128, 2])
            S.dma('sp', msk[:], I['msk'][:, :], w=['msk'])
            Vh = sb(st, 'Vh', [128, 22, 132], BF16)
            qf = sb(st, 'qf', [128, 512])
            t1 = sb(st, 't1', [128, 512])
            t2 = sb(st, 't2', [128, 512])
            qfb = sb(st, 'qfb', [128, 512])
            t1b = sb(st, 't1b', [128, 512])
            t2b = sb(st, 't2b', [128, 512])
            Es = [[sb(st, f'E{g_}_{s}', [128, 18, 512], BF16) for s in range(2)] for g_ in range(1)]
            o1g = sb(st, 'o1g', [128, 4, 132])
            o2g = sb(st, 'o2g', [128, 4, 132])
            rr = sb(st, 'rr', [128, 4, 4])
            gT = sb(st, 'gT', [128, 512], BF16)
            ong = qf[:, :].rearrange("p (a b) -> p a b", b=128)
            sqg = t1[:, :].rearrange("p (a b) -> p a b", b=128)
            kvo = t1[:, 0:256].rearrange("p (a b) -> p a b", b=128)
            kvo2 = t1[:, 256:512].rearrange("p (a b) -> p a b", b=128)
            ck = kvo2
            lamt = qf[:, 0:256].rearrange("p (a b) -> p a b", b=64)
            lamw = qf[:, 256:384].rearrange("p (a b) -> p a b", b=64)
            vf = t2
            S.dma('sp', lamt[:], I['lam_bc'][:, j, :, :], w=['qf'])
            S.tt('dve', lamw[:, 0, :], lamt[:, 0, :], lamt[:, 1, :], ALU.mult, r=['qf'], w=['qf'])
            S.tt('dve', lamw[:, 1, :], lamt[:, 2, :], lamt[:, 3, :], ALU.mult, r=['qf'], w=['qf'])
            w_ = 64
            while w_ > 1:
                w_ //= 2
                S.tt('dve', lamw[:, :, 0:w_], lamw[:, :, 0:w_], lamw[:, :, w_:2 * w_], ALU.add, r=['qf'], w=['qf'])
            S.copy('dve', lams[:, 0:1], lamw[:, 0, 0:1], r=['qf'], w=['lams'])
            S.copy('dve', lams[:, 1:2], lamw[:, 1, 0:1], r=['qf'], w=['lams'])
            S.act(lams[:, 0:2], lams[:, 0:2], AF.Exp, r=['lams'], w=['lams'])
            S.tt('dve', lams[:, 2:3], lams[:, 0:1], lams[:, 1:2], ALU.subtract, r=['lams'], w=['lams'])
            S.ts('dve', lams[:, 2:3], lams[:, 2:3], lam_init, None, ALU.add, r=['lams'], w=['lams'])

            wqs = [sb(st, f'wq{b_}', [128, KC, 128], BF16) for b_ in range(2)]
            wks = [sb(st, f'wk{b_}', [128, KC, 128], BF16) for b_ in range(2)]
            wvs = [sb(st, f'wv{b_}', [128, KC, 128], BF16) for b_ in range(2)]
            S.memset('dve', Vh[:, :, 128:129], 1.0, w=['Vh'])
            Wqkv = I['at_w_qkv']
            import os
            SK = set(os.environ.get('ATT_SKIP', '').split(','))
            NH = int(os.environ.get('ATT_HEADS', '8'))
            def ldw(hd_):
                b_ = hd_ % 2
                for q_, (wl, nm_) in enumerate(((wqs, 'wq'), (wks, 'wk'), (wvs, 'wv'))):
                    S.dma('pool', wl[b_][:, :, :],
                          Wqkv[j, :, q_ * D + hd_ * 128:q_ * D + (hd_ + 1) * 128].rearrange("(k p) n -> p k n", p=128),
                          w=[f'{nm_}{b_}'])
            ldw(0)
            for hd in range(NH):
                if hd + 1 < NH:
                    ldw(hd + 1)
                wq, wk_, wv = wqs[hd % 2], wks[hd % 2], wvs[hd % 2]
                WQK, WKK, WVK = f'wq{hd % 2}', f'wk{hd % 2}', f'wv{hd % 2}'
                for wi_, wsrc, wkey_ in ((0, wq, WQK), (1, wk_, WKK)):
                    for s2 in range(2):
                        for a2 in range(2):
                            b0 = s2 * 64 + a2 * 32
                            S.ts('dve', wrot[wi_][:, :, b0:b0 + 16], wsrc[:, :, b0 + 16:b0 + 32], -1.0, None, ALU.mult,
                                 r=[wkey_], w=['wrot'])
                            S.copy('dve', wrot[wi_][:, :, b0 + 16:b0 + 32], wsrc[:, :, b0:b0 + 16], r=[wkey_], w=['wrot'])
                if 'cache' in SK:
                    pass
                for a_ in range(2):
                    S.dma('pool', ck[:, a_, :], I['cache_k'][j, hd, a_ * 128:(a_ + 1) * 128, :], w=['t1'])
                for a_ in range(2):
                    S.dma('pool', Vh[:, a_, 0:128], I['cache_v'][j, hd, a_ * 128:(a_ + 1) * 128, :], w=['Vh'])
                for a in range(2):
                    S.tr(ps[4][:, a * 128:(a + 1) * 128], ck[:, a, :], ident[:], r=['t1', 'ident'], w=['ps4'])
                S.ts('dve', kT[:, 0:256], ps[4][:, 0:256], msk[:, 0:1], None, ALU.mult, r=['ps4', 'msk'], w=['kT'])
                S.ts('dve', kT2[:, 0:256], ps[4][:, 0:256], msk[:, 1:2], None, ALU.mult, r=['ps4', 'msk'], w=['kT'])
                for ci, (t0, n, lz, rz, cond) in enumerate(CH):
                    if cond == 0 and 'rope' not in SK:
                        rc, rs = rcts[ci % 2], rsts[ci % 2]
                        RK = f'rope{ci % 2}'
                        S.dma('pool', rc[:, :n], I['ropec'][:, t0:t0 + n], w=[RK + 'c'])
                        S.dma('pool', rs[:, :n], I['ropes'][:, t0:t0 + n], w=[RK + 's'])
                    for which, wt, wkey, dst, doff in ((0, wq, WQK, qT, 0), (1, wk_, WKK, None, 256)):
                        qf_, t1_, t2_ = (qf, t1, t2) if which == 0 else (qfb, t1b, t2b)
                        QK_, T1K, T2K = ('qf', 't1', 't2') if which == 0 else ('qfb', 't1b', 't2b')
                        prp = 2 + which
                        for kc in range(KC):
                            S.mm(ps[which][:, :n], wt[:, kc, :], h_all[:, kc, t0:t0 + n], kc == 0, kc == KC - 1,
                                 r=[wkey, f'h{t0}'], w=[PSK[which]], signal=(kc == KC - 1))
                        dk = 'qT' if which == 0 else 'kT'
                        if cond == 0 and 'rope' not in SK:
                            S.copy('act', qf_[:, :n], ps[which][:, :n], r=[PSK[which]], w=[QK_])
                            for kc in range(KC):
                                S.mm(ps[prp][:, :n], wrot[which][:, kc, :], h_all[:, kc, t0:t0 + n], kc == 0,
                                     kc == KC - 1, r=['wrot', f'h{t0}'], w=[PSK[prp]], signal=(kc == KC - 1))
                            S.tt('dve', t1_[:, :n], qf_[:, :n], rc[:, :n], ALU.mult, r=[QK_, RK + 'c'], w=[T1K])
                            S.tt('dve', t2_[:, :n], ps[prp][:, :n], rs[:, :n], ALU.mult, r=[PSK[prp], RK + 's'],
                                 w=[T2K])
                            if which == 0:
                                S.tt('dve', qT[:, t0:t0 + n], t1_[:, :n], t2_[:, :n], ALU.add, r=[T1K, T2K], w=[dk])
                            else:
                                S.tt('dve', t1_[:, :n], t1_[:, :n], t2_[:, :n], ALU.add, r=[T1K, T2K], w=[T1K])
                                S.ts('dve', kT[:, doff + t0:doff + t0 + n], t1_[:, :n], msk[:, 0:1], None, ALU.mult,
                                     r=[T1K, 'msk'], w=[dk])
                                S.ts('dve', kT2[:, doff + t0:doff + t0 + n], t1_[:, :n], msk[:, 1:2], None, ALU.mult,
                                     r=[T1K, 'msk'], w=[dk])
                        else:
                            if which == 0:
                                S.copy('act', qT[:, t0:t0 + n], ps[which][:, :n], r=[PSK[which]], w=[dk])
                            else:
                                S.ts('dve', kT[:, doff + t0:doff + t0 + n], ps[which][:, :n], msk[:, 0:1], None,
                                     ALU.mult, r=[PSK[which], 'msk'], w=[dk])
                                S.ts('dve', kT2[:, doff + t0:doff + t0 + n], ps[which][:, :n], msk[:, 1:2], None,
                                     ALU.mult, r=[PSK[which], 'msk'], w=[dk])
                for ci, (t0, n, lz, rz, cond) in enumerate(CH):
                    nt_ = n // 128
                    tt0 = t0 // 128
                    for kc in range(KC):
                        S.mm(ps[3][:, :n], wv[:, kc, :], h_all[:, kc, t0:t0 + n], kc == 0, kc == KC - 1,
                             r=[WVK, f'h{t0}'], w=['ps3'], signal=(kc == KC - 1))
                    S.copy('act', vf[:, :n], ps[3][:, :n], r=['ps3'], w=['t2'])
                    V2 = int(os.environ.get('V2_STOP', '9'))
                    if V2 < 2:
                        continue
                    for a_ in range(nt_):
                        S.tr(ps[5][:, a_ * 128:(a_ + 1) * 128], vf[:, a_ * 128:(a_ + 1) * 128], ident[:],
                             r=['t2', 'ident'], w=['ps5'])
                    if V2 < 3:
                        continue
                    S.copy('act', Vh[:, 2 + tt0:2 + tt0 + nt_, 0:128],
                           ps[5][:, 0:n].rearrange("p (a b) -> p a b", b=128), r=['ps5'], w=['Vh'])
                    if V2 < 4:
                        continue
                    if cond == 1:
                        sq_ = (t0 - 2048) // 256
                        S.copy('dve', kvo[:, :, :], ps[5][:, 0:256].rearrange("p (a b) -> p a b", b=128),
                               r=['ps5'], w=['t1'])
                        if V2 >= 5:
                            S.dma('sp', O['nv'][sq_, j, hd, :, :].rearrange("(a p) d -> p a d", p=128), kvo[:, :, :],
                                  r=['t1'], w=[f'nv{hd}_{sq_}'])
                        if V2 < 6:
                            continue
                        for kc in range(KC):
                            S.mm(ps[3][:, :n], wk_[:, kc, :], h_all[:, kc, t0:t0 + n], kc == 0, kc == KC - 1,
                                 r=[WKK, f'h{t0}'], w=['ps3'], signal=(kc == KC - 1))
                        S.copy('act', vf[:, :n], ps[3][:, :n], r=['ps3'], w=['t2'])
                        if V2 < 7:
                            continue
                        for a_ in range(nt_):
                            S.tr(ps[5][:, a_ * 128:(a_ + 1) * 128], vf[:, a_ * 128:(a_ + 1) * 128], ident[:],
                                 r=['t2', 'ident'], w=['ps5'])
                        if V2 < 8:
                            continue
                        S.copy('dve', kvo2[:, :, :], ps[5][:, 0:256].rearrange("p (a b) -> p a b", b=128),
                               r=['ps5'], w=['t1'])
                        if V2 < 9:
                            continue
                        S.dma('sp', O['nk'][sq_, j, hd, :, :].rearrange("(a p) d -> p a d", p=128), kvo2[:, :, :],
                              r=['t1'], w=[f'nk{hd}_{sq_}'])
                groups = [(qc * 512, 512, list(range(18)), 0) for qc in range(4)]
                groups += [(2048, 256, [18, 19], 1), (2304, 256, [20, 21], 1)]
                import os
                LV = int(os.environ.get('ATT_LEVEL', '9'))
                if LV < 2:
                    groups = []
                pendA = [None]
                for gi_, (q0, nq, ktl, isp) in enumerate(groups):
                    E = Es[0]
                    EK = ['E0_0', 'E0_1']
                    for ki, kt in enumerate(ktl):
                        kcol = kt * 128
                        for s_ in range(2):
                            pp = ps[s_ * 2 + (ki % 2)]
                            pk = PSK[s_ * 2 + (ki % 2)]
                            ksrc = kT if s_ == 0 else kT2
                            S.mm(pp[:, :nq], ksrc[:, kcol:kcol + 128], qT[:, q0:q0 + nq], True, True,
                                 r=['kT', 'qT'], w=[pk])
                            S.act(E[s_][:, ki, :nq], pp[:, :nq], AF.Exp, scale=0.125, r=[pk], w=[EK[s_]])
                    if pendA[0] is not None:
                        pendA[0]()
                        pendA[0] = None
                    nqt = nq // 128 if LV >= 3 else 0
                    for qt in range(nqt):
                        for s_, ot, ok_ in ((0, o1g, 'o1'), (1, o2g, 'o2')):
                            pp = ps[4 + s_]
                            for ki, kt in enumerate(ktl):
                                S.mm(pp[:, 0:129], E[s_][:, ki, qt * 128:(qt + 1) * 128], Vh[:, kt, 0:129], ki == 0,
                                     ki == len(ktl) - 1, r=[EK[s_], 'Vh'], w=[PSK[4 + s_]],
                                     signal=(ki == len(ktl) - 1))
                            S.copy('act' if s_ == 0 else 'dve', ot[:, qt, 0:129], pp[:, 0:129], r=[PSK[4 + s_]],
                                   w=[ok_])
                    if nqt:
                        S.recip(rr[:, 0, 0:nqt], o1g[:, 0:nqt, 128], r=['o1'], w=['rr'])
                        S.recip(rr[:, 1, 0:nqt], o2g[:, 0:nqt, 128], r=['o2'], w=['rr'])
                        S.ts('dve', rr[:, 1, 0:nqt], rr[:, 1, 0:nqt], lams[:, 2:3], None, ALU.mult, r=['rr', 'lams'],
                             w=['rr'])
                        for qt in range(nqt):
                            S.ts('dve', o2g[:, qt, 0:128], o2g[:, qt, 0:128], rr[:, 1, qt:qt + 1], None, ALU.mult,
                                 r=['o2', 'rr'], w=['o2'])
                            S.stt('dve', o1g[:, qt, 0:128], o1g[:, qt, 0:128], rr[:, 0, qt:qt + 1], o2g[:, qt, 0:128],
                                  ALU.mult, ALU.subtract, r=['o1', 'o2', 'rr'], w=['o1'])
                        S.act(sqg[:, 0:nqt, :], o1g[:, 0:nqt, 0:128], AF.Square, r=['o1'], w=['t1'])
                        w_ = 128
                        while w_ > 1:
                            w_ //= 2
                            S.tt('dve', sqg[:, 0:nqt, 0:w_], sqg[:, 0:nqt, 0:w_], sqg[:, 0:nqt, w_:2 * w_], ALU.add,
                                 r=['t1'], w=['t1'])
                        S.ts('dve', rr[:, 2, 0:nqt], sqg[:, 0:nqt, 0], 1.0 / 128, EPS, ALU.mult, ALU.add, r=['t1'],
                             w=['rr2'])
                        S.act(rr[:, 2, 0:nqt], rr[:, 2, 0:nqt], AF.Sqrt, r=['rr2'], w=['rr2'])
                        S.recip(rr[:, 3, 0:nqt], rr[:, 2, 0:nqt], r=['rr2'], w=['rr2'])
                        S.ts('dve', rr[:, 3, 0:nqt], rr[:, 3, 0:nqt], 1.0 - lam_init, None, ALU.mult, r=['rr2'],
                             w=['rr2'])
                        for qt in range(nqt):
                            S.stt('dve', ong[:, qt, :], o1g[:, qt, 0:128], rr[:, 3, qt:qt + 1], subw[:], ALU.mult,
                                  ALU.mult, r=['o1', 'rr2', 'subw'], w=['qf'])

                    def mk_fin(nqt=nqt, nq=nq, q0=q0, hd=hd):
                        def f():
                            for qt in range(nqt):
                                S.tr(ps[6][:, qt * 128:(qt + 1) * 128], ong[:, qt, :], ident[:], r=['qf', 'ident'],
                                     w=['ps6'])
                            if nqt:
                                S.copy('act', gT[:, 0:nq], ps[6][:, 0:nq], r=['ps6'], w=['gT'])
                            S.dma('sp', Gd[hd, :, q0:q0 + nq], gT[:, :nq], r=['gT'], w=[f'G{hd}_{q0}'])
                        return f
                    pendA[0] = mk_fin()
                if pendA[0] is not None:
                    pendA[0]()
                    pendA[0] = None
            S.barrier()
        if 'proj' not in os.environ.get('ATT_SKIP', ''):
            proj_out(i, 0, Gd, KC, I['at_w_out'][j], None)

    def sin_rr(dst, src_ps, n, fr, fb, tmpt, keys_r, key_w):
        S.act(tmpt[:, :n], src_ps, AF.Identity, bias=fb, scale=fr, r=keys_r, w=['sintmp'])
        for _ in range(2):
            S.op('dve', lambda e: e.tensor_single_scalar(out=tmpt[:, 512:512 + n], in_=tmpt[:, :n], scalar=-math.pi,
                                                         op=ALU.is_lt), r=['sintmp'], w=['sinm'])
            S.stt('dve', tmpt[:, :n], tmpt[:, 512:512 + n], 2 * math.pi, tmpt[:, :n], ALU.mult, ALU.add,
                  r=['sinm', 'sintmp'], w=['sintmp'])
            S.op('dve', lambda e: e.tensor_single_scalar(out=tmpt[:, 512:512 + n], in_=tmpt[:, :n], scalar=math.pi,
                                                         op=ALU.is_gt), r=['sintmp'], w=['sinm'])
            S.stt('dve', tmpt[:, :n], tmpt[:, 512:512 + n], -2 * math.pi, tmpt[:, :n], ALU.mult, ALU.add,
                  r=['sinm', 'sintmp'], w=['sintmp'])
        S.act(dst, tmpt[:, :n], AF.Sin, r=['sintmp'], w=[key_w])

    def hyena(i):
        j = i // 2
        with ExitStack() as st:
            h_all = norm_to_hall(st, i, 0)
            bin_ = sb(st, 'bin', [128, 24])
            wsh = sb(st, 'wsh', [128, 3, 24])
            bsh = sb(st, 'bsh', [128, 24])
            S.dma('sp', bin_[:], I['hy_b_inT'][:, j, :], w=['cvec'])
            S.dma('sp', wsh[:], I['hy_w_shortT'][:, j, :, :], w=['cvec'])
            S.dma('sp', bsh[:], I['hy_b_shortT'][:, j, :], w=['cvec'])
            bsum = sb(st, 'bsum', [128, 24])
            bc0 = sb(st, 'bc0', [128, 24])
            bc2 = sb(st, 'bc2', [128, 24])
            S.tt('dve', bsum[:], wsh[:, 0, :], wsh[:, 1, :], ALU.add, r=['cvec'], w=['cvec2'])
            S.tt('dve', bsum[:], bsum[:], wsh[:, 2, :], ALU.add, r=['cvec', 'cvec2'], w=['cvec2'])
            S.tt('dve', bsum[:], bsum[:], bin_[:], ALU.mult, r=['cvec', 'cvec2'], w=['cvec2'])
            S.tt('dve', bsum[:], bsum[:], bsh[:], ALU.add, r=['cvec', 'cvec2'], w=['cvec2'])
            S.tt('dve', bc0[:], bin_[:], wsh[:, 0, :], ALU.mult, r=['cvec'], w=['cvec3'])
            S.ts('dve', bc0[:], bc0[:], -1.0, None, ALU.mult, r=['cvec3'], w=['cvec3'])
            S.tt('dve', bc2[:], bin_[:], wsh[:, 2, :], ALU.mult, r=['cvec'], w=['cvec4'])
            S.ts('dve', bc2[:], bc2[:], -1.0, None, ALU.mult, r=['cvec4'], w=['cvec4'])
            w3s = [[sb(st, f'w3_{b}_{d_}', [128, KC, 128], BF16) for b in range(3)] for d_ in range(2)]
            ub = [sb(st, f'ub{b}', [128, 514]) for b in range(3)]
            cc_ = [sb(st, f'cc{b}', [128, 512]) for b in range(3)]
            zts = [sb(st, f'zt{b_}', [128, 512]) for b_ in range(2)]
            zit = [0]
            pend = [None]
            zTt = sb(st, 'zTt', [128, 4, 128], BF16)
            Win = I['hy_w_in']
            def ldw3(cc_):
                for b in range(3):
                    S.dma('pool', w3s[cc_ % 2][b][:, :, :],
                          Win[j, :, b * D + cc_ * 128:b * D + (cc_ + 1) * 128].rearrange("(k p) n -> p k n", p=128),
                          w=[f'w3_{b}_{cc_ % 2}'])
            ldw3(0)
            for cc in range(8):
                if cc + 1 < 8:
                    ldw3(cc + 1)
                w3 = w3s[cc % 2]
                for ci, (t0, n, lz, rz, cond) in enumerate(CH):
                    for b in range(3):
                        lin_chunk(h_all, w3[b], f'w3_{b}_{cc % 2}', 0, t0, n, lz, rz, 2 * b, 2 * b + 1)
                        col = b * 8 + cc
                        conv3f(ps[2 * b][:, :n], ps[2 * b + 1], n, lz, rz,
                               wsh[:, 0, col:col + 1], wsh[:, 1, col:col + 1], wsh[:, 2, col:col + 1],
                               bsum[:, col:col + 1], cc_[b], PSK[2 * b], PSK[2 * b + 1], f'cc{b}')
                        if lz:
                            S.ts('dve', cc_[b][:, 0:1], cc_[b][:, 0:1], bc0[:, col:col + 1], None, ALU.add,
                                 r=[f'cc{b}', 'cvec3'], w=[f'cc{b}'])
                        if rz:
                            S.ts('dve', cc_[b][:, n - 1:n], cc_[b][:, n - 1:n], bc2[:, col:col + 1], None, ALU.add,
                                 r=[f'cc{b}', 'cvec4'], w=[f'cc{b}'])
                    S.dma('sp', X0d[cc, :, t0:t0 + n], cc_[0][:, :n], r=['cc0'], w=[f'X0{cc}_{t0}'])
                    zb_ = zit[0] % 2
                    zit[0] += 1
                    zt = zts[zb_]
                    S.tt('dve', zt[:, :n], cc_[1][:, :n], cc_[2][:, :n], ALU.mult, r=['cc1', 'cc2'], w=[f'zt{zb_}'])
                    S.dma('sp', Zd[cc, :, t0:t0 + n], zt[:, :n], r=[f'zt{zb_}'], w=[f'Z{cc}_{t0}'])
                    if pend[0] is not None:
                        pend[0]()

                    def mk_tr(zb_=zb_, n=n, t0=t0, cc=cc):
                        def f():
                            zt_ = zts[zb_]
                            for tt in range(n // 128):
                                S.tr(ps[6][:, tt * 128:(tt + 1) * 128], zt_[:, tt * 128:(tt + 1) * 128], ident[:],
                                     r=[f'zt{zb_}', 'ident'], w=['ps6'])
                            S.copy('act', zTt[:, 0:n // 128, :], ps[6][:, 0:n].rearrange("p (a b) -> p a b", b=128),
                                   r=['ps6'], w=['zTt'])
                            S.dma('sp', ZTd[t0:t0 + n, cc * 128:(cc + 1) * 128].rearrange("(a p) c -> p a c", p=128),
                                  zTt[:, 0:n // 128, :], r=['zTt'], w=[f'ZT{cc}_{t0}'])
                        return f
                    pend[0] = mk_tr()
            if pend[0] is not None:
                pend[0]()
            S.barrier()
        for (L, nm, nseq, tbase) in ((LS, 's', 1, 0), (LP, 'p', 2, 2048)):
            NT = L // 128
            with ExitStack() as st:
                posT = sb(st, 'posT', [33, L])
                fw1 = sb(st, 'fw1', [33, 64])
                fw2 = sb(st, 'fw2', [64, 64])
                fvec = sb(st, 'fvec', [64, 6])
                hid1 = sb(st, 'hid1', [64, L])
                hid2 = sb(st, 'hid2', [64, L], BF16)
                sint = sb(st, 'sint', [64, 1024])
                S.dma('sp', posT[:], I['posT' + nm][:, :], w=['posT'])
                S.dma('sp', fw1[:], I['hy_f_w1'][j, :, :], w=['fw'])
                S.dma('sp', fw2[:], I['hy_f_w2'][j, :, :], w=['fw'])
                S.dma('sp', fvec[:, 0:3], I['hy_fvecT'][j, :, :], w=['fvec'])
                S.tt('dve', fvec[:, 3:4], fvec[:, 0:1], fvec[:, 1:2], ALU.mult, r=['fvec'], w=['fvec'])
                S.tt('dve', fvec[:, 4:5], fvec[:, 2:3], fvec[:, 1:2], ALU.mult, r=['fvec'], w=['fvec'])
                for c0 in range(0, L, 512):
                    n = min(512, L - c0)
                    S.mm(ps[0][0:64, :n], fw1[:, :], posT[:, c0:c0 + n], True, True, r=['fw', 'posT'], w=['ps0'])
                    sin_rr(hid1[:, c0:c0 + n], ps[0][0:64, :n], n, fvec[:, 1:2], fvec[:, 3:4], sint, ['ps0', 'fvec'], 'hid1')
                    S.mm(ps[1][0:64, :n], fw2[:, :], hid1[:, c0:c0 + n], True, True, r=['fw', 'hid1'], w=['ps1'])
                    sin_rr(hid2[:, c0:c0 + n], ps[1][0:64, :n], n, fvec[:, 1:2], fvec[:, 4:5], sint, ['ps1', 'fvec'], 'hid2')
                fw3 = sb(st, 'fw3', [64, 2 * D], BF16)
                S.dma('pool', fw3[:], I['hy_f_w3'][j, :, :], w=['fw3'])
                hs = sb(st, 'hs', [128, NT, 256], BF16)
                hd_ = sb(st, 'hd', [128, NT, 256], BF16)
                dcts = [sb(st, f'dct{b_}', [128, 256]) for b_ in range(2)]
                absd = sb(st, 'absd', [128, D])
                tlt = sb(st, 'tlt', [128, NT])
                S.dma('sp', absd[:], I['absd'][:, :], w=['absd'])
                S.dma('sp', tlt[:], I['tl' + nm][:, :], w=['absd'])
                hfs = [sb(st, f'hf{b_}', [128, 2, 256]) for b_ in range(2)]
                zT = sb(st, 'zT', [128, NT, nseq * 256], BF16)
                FTs = [sb(st, f'FT{b_}', [128, 2, NT, 128], BF16) for b_ in range(2)]
                AB = sb(st, 'AB', [128, NT, 2, nseq * 256], BF16)
                Kfs = [sb(st, f'Kf{b_}', [128, 2, 256]) for b_ in range(2)]
                m_ = [sb(st, f'm{b}', [128, 256]) for b in range(4)]
                ITs = [sb(st, f'IT{b_}', [128, 2, NT, 256], BF16) for b_ in range(2)]
                yvs = [sb(st, f'yv{b_}', [128, 256]) for b_ in range(2)]
                x0ts = [sb(st, f'x0t{b_}', [128, 256]) for b_ in range(2)]
                zzs = [sb(st, f'zz{b_}', [128, 256]) for b_ in range(2)]
                gts = [sb(st, f'gt{b_}', [128, 256], BF16) for b_ in range(2)]
                inv_it = [0]

                def ldxz(rd_, tc_, sq__, dc_, b_, L=L, tbase=tbase):
                    ch_ = rd_ * 2 + dc_
                    tg_ = tbase + sq__ * L + tc_ * 256
                    S.dma('sp', x0ts[b_][:], X0d[ch_, :, tg_:tg_ + 256], r=['X0all'], w=[f'x0t{b_}'])
                    S.dma('sp', zzs[b_][:], Zd[ch_, :, tg_:tg_ + 256], r=['Zall'], w=[f'zz{b_}'])

                def inv_next(rd_, tc_, sq__, dc_, L=L, nseq=nseq):
                    if dc_ == 0:
                        return (rd_, tc_, sq__, 1)
                    if sq__ + 1 < nseq:
                        return (rd_, tc_, sq__ + 1, 0)
                    if tc_ + 1 < L // 256:
                        return (rd_, tc_ + 1, 0, 0)
                    return None
                dbt = sb(st, 'dbt', [128, 8])
                S.dma('sp', dbt[:], I['hy_d_biasT'][:, j, :], w=['dbt'])
                for rd in range(4):
                    c0 = rd * 256
                    for tt in range(NT):
                        dct, hf = dcts[tt % 2], hfs[tt % 2]
                        dk_, hk_ = f'dct{tt % 2}', f'hf{tt % 2}'
                        pa, pb = 2 * (tt % 2), 2 * (tt % 2) + 1
                        S.mm(ps[pa][:, 0:256], hid2[:, tt * 128:(tt + 1) * 128], fw3[:, c0:c0 + 256], True, True,
                             r=['hid2', 'fw3'], w=[PSK[pa]])
                        S.mm(ps[pb][:, 0:256], hid2[:, tt * 128:(tt + 1) * 128], fw3[:, D + c0:D + c0 + 256], True, True,
                             r=['hid2', 'fw3'], w=[PSK[pb]])
                        S.act(dct[:, :], absd[:, c0:c0 + 256], AF.Exp, scale=tlt[:, tt:tt + 1], r=['absd'], w=[dk_])
                        S.tt('dve', hf[:, 0, :], ps[pa][:, 0:256], dct[:, :], ALU.mult, r=[PSK[pa], dk_],
                             w=[hk_ + 'a'])
                        S.tt('dve', hf[:, 1, :], ps[pb][:, 0:256], dct[:, :], ALU.mult, r=[PSK[pb], dk_],
                             w=[hk_ + 'b'])
                        if tt == 0:
                            S.memset('dve', hf[0:1, 1, :], 0.0, r=[hk_ + 'b'], w=[hk_ + 'b'])
                        S.tt('dve', hs[:, tt, :], hf[:, 0, :], hf[:, 1, :], ALU.add, r=[hk_ + 'a', hk_ + 'b'], w=['hs'])
                        S.tt('pool', hd_[:, tt, :], hf[:, 0, :], hf[:, 1, :], ALU.subtract, r=[hk_ + 'a', hk_ + 'b'],
                             w=['hd'])
                    for sq_ in range(nseq):
                        tb = tbase + sq_ * L
                        S.dma('sp', zT[:, :, sq_ * 256:(sq_ + 1) * 256],
                              ZTd[tb:tb + L, c0:c0 + 256].rearrange("(a p) c -> p a c", p=128),
                              r=['ZTall'], w=['zT'])
                    for fc in range(NT):
                        FT = FTs[fc % 2]
                        FTK = f'FT{fc % 2}'
                        S.dma('sp', FT[:, :, :, :], I['FT' + nm][fc].rearrange("p (c a f) -> p c a f", c=2, a=NT),
                              w=[FTK])
                        nz = nseq * 256
                        pb_ = 4 * (fc % 2)
                        Kf = Kfs[fc % 2]
                        KfK = f'Kf{fc % 2}'
                        for cs in range(2):
                            for tt in range(NT):
                                S.mm(ps[pb_ + cs][:, :nz], FT[:, cs, tt, :], zT[:, tt, :], tt == 0, tt == NT - 1,
                                     r=[FTK, 'zT'], w=[PSK[pb_ + cs]], signal=(tt == NT - 1))
                            src = hs if cs == 0 else hd_
                            for tt in range(NT):
                                S.mm(ps[pb_ + 2 + cs][:, :256], FT[:, cs, tt, :], src[:, tt, :], tt == 0, tt == NT - 1,
                                     r=[FTK, 'hs', 'hd'], w=[PSK[pb_ + 2 + cs]], signal=(tt == NT - 1))
                            S.copy('act', Kf[:, cs, :], ps[pb_ + 2 + cs][:, :256], r=[PSK[pb_ + 2 + cs]], w=[KfK])
                        for sq_ in range(nseq):
                            zs = slice(sq_ * 256, (sq_ + 1) * 256)
                            S.tt('dve', m_[0][:], Kf[:, 0, :], ps[pb_][:, zs], ALU.mult, r=[KfK, PSK[pb_]], w=['m0'])
                            S.tt('dve', m_[1][:], Kf[:, 1, :], ps[pb_ + 1][:, zs], ALU.mult, r=[KfK, PSK[pb_ + 1]],
                                 w=['m1'])
                            S.tt('dve', m_[2][:], Kf[:, 0, :], ps[pb_ + 1][:, zs], ALU.mult, r=[KfK, PSK[pb_ + 1]],
                                 w=['m2'])
                            S.tt('dve', m_[3][:], Kf[:, 1, :], ps[pb_][:, zs], ALU.mult, r=[KfK, PSK[pb_]], w=['m3'])
                            S.tt('pool', AB[:, fc, 0, zs], m_[0][:], m_[1][:], ALU.subtract, r=['m0', 'm1'], w=['AB'])
                            S.tt('pool', AB[:, fc, 1, zs], m_[2][:], m_[3][:], ALU.add, r=['m2', 'm3'], w=['AB'])
                    inv_it[0] = 0
                    def ldit(tc_):
                        S.dma('sp', ITs[tc_ % 2][:, :, :, :],
                              I['IT' + nm][tc_].rearrange("p (c a t) -> p c a t", c=2, a=NT), w=[f'IT{tc_ % 2}'])
                    ldit(0)
                    for tc in range(L // 256):
                        IT = ITs[tc % 2]
                        ITK = f'IT{tc % 2}'
                        if tc + 1 < L // 256:
                            ldit(tc + 1)
                        for sq_ in range(nseq):
                            for dc in range(2):
                                ch = rd * 2 + dc
                                pp = ps[4 + dc]
                                cnt = 0
                                for fc in range(NT):
                                    for cs in range(2):
                                        S.mm(pp[:, :256], AB[:, fc, cs, sq_ * 256 + dc * 128:sq_ * 256 + (dc + 1) * 128],
                                             IT[:, cs, fc, :], cnt == 0, cnt == 2 * NT - 1, r=['AB', ITK],
                                             w=[PSK[4 + dc]], signal=(cnt == 2 * NT - 1))
                                        cnt += 1
                                tg = tbase + sq_ * L + tc * 256
                                b_ = inv_it[0] % 2
                                x0t, zz, yv, gt = x0ts[b_], zzs[b_], yvs[b_], gts[b_]
                                if inv_it[0] == 0:
                                    ldxz(rd, tc, sq_, dc, 0)
                                nxt = inv_next(rd, tc, sq_, dc)
                                if nxt is not None:
                                    ldxz(*nxt, (inv_it[0] + 1) % 2)
                                S.ts('dve', zz[:], zz[:], dbt[:, ch:ch + 1], None, ALU.mult, r=[f'zz{b_}', 'dbt'],
                                     w=[f'zz{b_}'])
                                S.stt('dve', yv[:], pp[:, :256], 1.0 / L, zz[:], ALU.mult, ALU.add,
                                      r=[PSK[4 + dc], f'zz{b_}'], w=[f'yv{b_}'])
                                S.tt('pool', gt[:], yv[:], x0t[:], ALU.mult, r=[f'yv{b_}', f'x0t{b_}'], w=[f'gt{b_}'])
                                S.dma('sp', Gd[ch, :, tg:tg + 256], gt[:], r=[f'gt{b_}'], w=[f'G{ch}_{tg}'])
                                inv_it[0] += 1
                S.barrier()
        with ExitStack() as st:
            bo = sb(st, 'bo', [128, 8])
            S.dma('sp', bo[:], I['hy_b_outT'][:, j, :], w=['bvec'])
            proj_out(i, 0, Gd, KC, I['hy_w_out'][j], bo)

    stage_in()
    for (kind, i) in prog:
        if kind == 'mix':
            if i % 2 == 0:
                hyena(i)
            else:
                attn(i)
        else:
            ffn(i)
    stage_out()
    S.barrier()

    with nc.Block() as block:
        def mk(name):
            def body(e):
                for it in S.q[name]:
                    if it[0] == 'w':
                        e.wait_ge(S.sems[it[1]], it[2])
                    else:
                        ins = it[1](e)
                        if it[2] is not None:
                            ins.then_inc(S.sems[it[2]], it[3])
            return body
        block.tensor(mk('pe'))
        block.scalar(mk('act'))
        block.vector(mk('dve'))
        block.gpsimd(mk('pool'))
        block.sync(mk('sp'))
    es.close()
    k.nops = S.nops
    return nc


def _T(v, nch):
    return np.ascontiguousarray(np.asarray(v, np.float32).reshape(nch, 128).T)


def _tables():
    tb = {}
    tb['ident'] = np.eye(128, dtype=np.float32)
    mk_ = np.zeros((128, 2), np.float32)
    mk_[:64, 0] = 1.0
    mk_[64:, 1] = 1.0
    tb['msk'] = mk_
    R = np.zeros((128, 128), np.float32)
    for s in range(2):
        for a in range(2):
            for f in range(16):
                p0 = s * 64 + a * 32 + f
                p1 = p0 + 16
                R[p1, p0] = -1.0
                R[p0, p1] = 1.0
    tb['rotm'] = R
    rows = LS // 64
    r = np.repeat(np.arange(rows, dtype=np.float32), 64)
    cidx = np.tile(np.arange(64, dtype=np.float32), rows)
    inv = (10000.0 ** (-np.arange(16, dtype=np.float32) / 16)).astype(np.float32)
    ar = r[:, None] * inv
    ac = cidx[:, None] * inv
    cos = np.concatenate([np.cos(ar), np.cos(ar), np.cos(ac), np.cos(ac)], -1).astype(np.float32)
    sin = np.concatenate([np.sin(ar), np.sin(ar), np.sin(ac), np.sin(ac)], -1).astype(np.float32)
    tb['ropec'] = np.ascontiguousarray(np.concatenate([cos, cos], -1).T)
    tb['ropes'] = np.ascontiguousarray(np.concatenate([sin, sin], -1).T)
    for L, nm in ((LS, 's'), (LP, 'p')):
        f = np.arange(L, dtype=np.float64)[:, None]
        n = np.arange(L, dtype=np.float64)[None, :]
        ang = np.pi * (2 * f + 1) * n / (2 * L)
        Cq, Sq = np.cos(ang), np.sin(ang)
        NT = L // 128
        CS = np.stack([Cq, Sq], 0).astype(np.float32)
        ft = CS.reshape(2, NT, 128, NT, 128).transpose(1, 4, 0, 3, 2)
        tb['FT' + nm] = np.ascontiguousarray(ft).reshape(NT, 128, 2 * L).astype(ml_dtypes.bfloat16)
        it = CS.reshape(2, NT, 128, L // 256, 256).transpose(3, 2, 0, 1, 4)
        tb['IT' + nm] = np.ascontiguousarray(it).reshape(L // 256, 128, 2 * NT * 256).astype(ml_dtypes.bfloat16)
        t = np.linspace(0.0, 1.0, L, dtype=np.float32)[:, None]
        w = (2.0 * np.float32(math.pi) * np.arange(L, dtype=np.float32)[:, None] / np.float32(L)).astype(np.float32)
        bands = np.linspace(1e-4, 15, 16, dtype=np.float32)
        z = np.concatenate([t, np.cos(bands * w), -np.sin(bands * w)], -1).astype(np.float32)
        tb['posT' + nm] = np.ascontiguousarray(z.T)
        mind = math.log(1e-2) / 1.5
        maxd = math.log(1e-2) / 0.3
        deltas = np.linspace(mind, maxd, D, dtype=np.float32)
        tb['tl' + nm] = np.ascontiguousarray((-t[:, 0]).reshape(L // 128, 128).T).astype(np.float32)
        tb['absd'] = np.ascontiguousarray(np.broadcast_to(np.abs(deltas)[None, :], (128, D))).astype(np.float32)
    return tb


_CACHE = {}


def _run(inputs, prog, dbg_x=None):
    key = tuple(prog)
    if key not in _CACHE:
        _CACHE[key] = build(prog)
    nc = _CACHE[key]
    tb = _tables()
    g = {k: np.asarray(v) for k, v in inputs.items()}
    shared = {
        'w_ada': g['w_ada'], 'b_adaT': np.ascontiguousarray(g['b_ada'].reshape(4, 48, 128).transpose(2, 0, 1)),
        'norm_wT': np.ascontiguousarray(g['norm_w'].reshape(4, 4, 8, 128).transpose(3, 0, 1, 2)),
        'hy_w_in': g['hy_w_in'],
        'hy_b_inT': np.ascontiguousarray(g['hy_b_in'].reshape(2, 24, 128).transpose(2, 0, 1)),
        'hy_w_shortT': np.ascontiguousarray(g['hy_w_short'].reshape(2, 3, 24, 128).transpose(3, 0, 1, 2)),
        'hy_b_shortT': np.ascontiguousarray(g['hy_b_short'].reshape(2, 24, 128).transpose(2, 0, 1)),
        'hy_f_w1': g['hy_f_w1'], 'hy_f_w2': g['hy_f_w2'], 'hy_f_w3': g['hy_f_w3'],
        'hy_fvecT': np.ascontiguousarray(np.stack([g['hy_f_b1'], g['hy_f_freq'], g['hy_f_b2']], -1)).astype(np.float32),
        'hy_d_biasT': np.ascontiguousarray(g['hy_d_bias'].reshape(2, 8, 128).transpose(2, 0, 1)),
        'hy_w_out': g['hy_w_out'],
        'hy_b_outT': np.ascontiguousarray(g['hy_b_out'].reshape(2, 8, 128).transpose(2, 0, 1)),
        'at_w_qkv': g['at_w_qkv'], 'at_w_out': g['at_w_out'],
        'lam_bc': np.ascontiguousarray(np.broadcast_to(
            np.stack([g['at_lambda_q1'], g['at_lambda_k1'], g['at_lambda_q2'], g['at_lambda_k2']], 1)[None],
            (128, 2, 4, 64))).astype(np.float32),
        'subln_bc': np.ascontiguousarray(np.broadcast_to(g['at_subln'][None], (128, 2, 128))).astype(np.float32),
        'ffn_w_up': g['ffn_w_up'],
        'ffn_w_dwT': np.ascontiguousarray(g['ffn_w_dw'].reshape(4, 3, 44, 128).transpose(3, 0, 1, 2)),
        'ffn_b_dwT': np.ascontiguousarray(g['ffn_b_dw'].reshape(4, 44, 128).transpose(2, 0, 1)),
        'ffn_w_down': g['ffn_w_down'],
    }
    shared.update(tb)
    in_maps = []
    for c in range(8):
        b = c // 4
        m = dict(shared)
        if dbg_x is not None:
            m['xin'] = dbg_x
        else:
            m['xin'] = np.ascontiguousarray(np.concatenate(
                [g['x_sample'][b], g['x_prompt'][2 * c], g['x_prompt'][2 * c + 1]], 0)).astype(np.float32)
        m['condT'] = np.ascontiguousarray(np.stack([_T(g['c'][b], 8), _T(g['c_ctx'], 8)], -1))
        m['cache_k'] = np.ascontiguousarray(g['cache_k'][b])
        m['cache_v'] = np.ascontiguousarray(g['cache_v'][b])
        in_maps.append(m)
    import os
    ncores = int(os.environ.get('K_NCORES', '8'))
    if os.environ.get('K_TRACE'):
        res = run_bass_kernel_spmd(nc, in_maps[:ncores], core_ids=list(range(ncores)), trace=True)
        print("EXEC_TIME_NS", res.exec_time_ns)
    else:
        res = run_bass_kernel_spmd(nc, in_maps[:ncores], core_ids=list(range(ncores)))
    return list(res.results) + [res.results[0]] * (8 - ncores)


FULL = [('mix', 0), ('ffn', 0), ('mix', 1), ('ffn', 1), ('mix', 2), ('ffn', 2), ('mix', 3), ('ffn', 3)]


def kernel(**inputs):
    r = _run(inputs, FULL)
    yp = np.zeros((16, 256, D), np.float32)
    ys = np.zeros((2, 2048, D), np.float32)
    nk = np.zeros((16, 2, 8, 256, 128), np.float32)
    nv = np.zeros((16, 2, 8, 256, 128), np.float32)
    for c in range(8):
        yo = np.asarray(r[c]['yout'])
        yp[2 * c] = yo[2048:2304]
        yp[2 * c + 1] = yo[2304:2560]
        if c % 4 == 0:
            ys[c // 4] = yo[:2048]
        nk[2 * c:2 * c + 2] = np.asarray(r[c]['nk'])
        nv[2 * c:2 * c + 2] = np.asarray(r[c]['nv'])
    return (yp, ys, nk, nv)
```
